# Optimizing a Trainium2 kernel written in Bass

```python
import math
import jax, jax.numpy as jnp
from jax import lax
import numpy as np

D_MODEL = 1024
BATCH = 8
SEQ = 4096
DEPTH = 2
DEC_BATCH = 16
DEC_SEQ = 64
PAST_LEN = 2048

CHUNK = 64
MLP_CHUNK = 128
D_A = D_MODEL
N_GROUPS = 4
GROUP_W = D_A // N_GROUPS
N_HEADS = 8
HEAD_DIM = 64
V_DIM = 2 * HEAD_DIM
QK_W = N_HEADS * 2 * HEAD_DIM
D_B = N_HEADS * V_DIM
Q_BLOCK = 128
EPS = 1e-6
IN_SIZES = (D_A, D_A, D_A, QK_W, QK_W, D_B, D_B, D_MODEL, D_MODEL)
D_IN = D_A * 3 + QK_W * 2 + D_B * 2 + D_MODEL * 2

kernel_name = "gated_chunkmlp_diffattn_stream_step"


def rmsnorm(x, g):
    xf = x.astype(jnp.float32)
    y = xf * lax.rsqrt(jnp.mean(xf * xf, axis=-1, keepdims=True) + EPS)
    return (y * g.astype(jnp.float32)).astype(x.dtype)


def split_in(z):
    idx = []
    acc = 0
    for s in IN_SIZES[:-1]:
        acc += s
        idx.append(acc)
    return jnp.split(z, idx, axis=-1)


def masked_spatial(w_s):
    tril = jnp.tril(jnp.ones((MLP_CHUNK, MLP_CHUNK), dtype=bool))
    return jnp.where(tril[None], w_s, jnp.zeros_like(w_s))


def chunk_mlp_prompt(v, w_s, b_s):
    B, S, _ = v.shape
    n = S // MLP_CHUNK
    vr = v.reshape(B, n, MLP_CHUNK, N_GROUPS, GROUP_W)
    s = jnp.einsum('gts,bnsgc->bntgc', masked_spatial(w_s), vr)
    s = s + jnp.transpose(b_s)[None, None, :, :, None]
    return s.reshape(B, S, D_A)


def chunk_mlp_sample(v, w_s, b_s):
    B, L, _ = v.shape
    ws = masked_spatial(w_s)[:, :L, :L]
    vr = v.reshape(B, L, N_GROUPS, GROUP_W)
    s = jnp.einsum('gts,bsgc->btgc', ws, vr) + jnp.transpose(b_s[:, :L])[None, :, :, None]
    return s.reshape(B, L, D_A)


def diff_attn_core(q, k, v, lam, mask):
    scale = HEAD_DIM ** -0.5
    s = jnp.einsum('bqhjd,bkhjd->bhjqk', q, k).astype(jnp.float32) * scale
    if mask is not None:
        s = jnp.where(mask[None, None, None], s, jnp.float32(-1e30))
    p = jax.nn.softmax(s, axis=-1)
    a = p[:, :, 0] - lam * p[:, :, 1]
    return jnp.einsum('bhqk,bkhe->bqhe', a.astype(v.dtype), v)


def diff_attn_prompt(q, k, v, lam):
    B, S = q.shape[0], q.shape[1]
    nb = S // Q_BLOCK
    qb = jnp.moveaxis(q.reshape(B, nb, Q_BLOCK, N_HEADS, 2, HEAD_DIM), 1, 0)
    kpos = jnp.arange(S)

    def one_block(args):
        q_blk, blk = args
        qpos = blk * Q_BLOCK + jnp.arange(Q_BLOCK)
        mask = (kpos[None, :] // CHUNK) <= (qpos[:, None] // CHUNK)
        return diff_attn_core(q_blk, k, v, lam, mask)

    o = lax.map(one_block, (qb, jnp.arange(nb)))
    return jnp.moveaxis(o, 0, 1).reshape(B, S, N_HEADS, V_DIM)


def mixer_layer(x, li, norm_g, w_in, w_s, b_s, v_norm_g, lam_q1, lam_k1, lam_q2, lam_k2,
                attn_norm_g, w_pa, w_pb, w_out, cache_k=None, cache_v=None):
    B, L, _ = x.shape
    h = rmsnorm(x, norm_g)
    a_u, a_v, a_z, b_q, b_k, b_v, b_z, g_a, g_b = split_in(h @ w_in)
    a_u = jax.nn.gelu(a_u)
    a_v = rmsnorm(jax.nn.gelu(a_v), v_norm_g)
    if cache_k is None:
        spatial = chunk_mlp_prompt(a_v, w_s, b_s)
    else:
        spatial = chunk_mlp_sample(a_v, w_s, b_s)
    y_a = jax.nn.silu(a_z) * (a_u * spatial)
    q = b_q.reshape(B, L, N_HEADS, 2, HEAD_DIM)
    k = b_k.reshape(B, L, N_HEADS, 2, HEAD_DIM)
    v = b_v.reshape(B, L, N_HEADS, V_DIM)
    lam_init = 0.8 - 0.6 * math.exp(-0.3 * li)
    lam = (jnp.exp(jnp.sum(lam_q1.astype(jnp.float32) * lam_k1.astype(jnp.float32)))
           - jnp.exp(jnp.sum(lam_q2.astype(jnp.float32) * lam_k2.astype(jnp.float32)))
           + lam_init)
    if cache_k is None:
        o = diff_attn_prompt(q, k, v, lam)
    else:
        P = cache_k.shape[1]
        k_all = jnp.concatenate([cache_k.reshape(B, P, N_HEADS, 2, HEAD_DIM), k], axis=1)
        v_all = jnp.concatenate([cache_v, v], axis=1)
        o = diff_attn_core(q, k_all, v_all, lam, None)
    o = rmsnorm(o, attn_norm_g) * (1.0 - lam_init)
    y_b = jax.nn.silu(b_z) * o.reshape(B, L, D_B)
    merged = jax.nn.sigmoid(g_a) * (y_a @ w_pa) + jax.nn.sigmoid(g_b) * (y_b @ w_pb)
    x = x + merged @ w_out
    return x, k.reshape(B, L, N_HEADS, 2 * HEAD_DIM), v, a_v


def setup_inputs(seed: int = 0) -> dict:
    key = jax.random.key(seed)
    ks = jax.random.split(key, 20)
    nrm = jax.random.normal
    f32 = jnp.float32
    return {
        "x_prompt": nrm(ks[0], (BATCH, SEQ, D_MODEL), f32),
        "x_sample": nrm(ks[1], (DEC_BATCH, DEC_SEQ, D_MODEL), f32),
        "cache_k": nrm(ks[2], (DEPTH, DEC_BATCH, PAST_LEN, N_HEADS, 2 * HEAD_DIM), f32),
        "cache_v": nrm(ks[3], (DEPTH, DEC_BATCH, PAST_LEN, N_HEADS, V_DIM), f32),
        "norm_g": 1.0 + 0.01 * nrm(ks[4], (DEPTH, D_MODEL), f32),
        "w_in": nrm(ks[5], (DEPTH, D_MODEL, D_IN), f32) * D_MODEL ** -0.5,
        "w_s": nrm(ks[6], (DEPTH, N_GROUPS, MLP_CHUNK, MLP_CHUNK), f32) * (0.5 * MLP_CHUNK ** -0.5),
        "b_s": 1.0 + 0.01 * nrm(ks[7], (DEPTH, N_GROUPS, MLP_CHUNK), f32),
        "v_norm_g": 1.0 + 0.01 * nrm(ks[8], (DEPTH, D_A), f32),
        "lam_q1": 0.1 * nrm(ks[9], (DEPTH, HEAD_DIM), f32),
        "lam_k1": 0.1 * nrm(ks[10], (DEPTH, HEAD_DIM), f32),
        "lam_q2": 0.1 * nrm(ks[11], (DEPTH, HEAD_DIM), f32),
        "lam_k2": 0.1 * nrm(ks[12], (DEPTH, HEAD_DIM), f32),
        "attn_norm_g": 1.0 + 0.01 * nrm(ks[13], (DEPTH, V_DIM), f32),
        "w_pa": nrm(ks[14], (DEPTH, D_A, D_MODEL), f32) * D_A ** -0.5,
        "w_pb": nrm(ks[15], (DEPTH, D_B, D_MODEL), f32) * D_B ** -0.5,
        "w_out": nrm(ks[16], (DEPTH, D_MODEL, D_MODEL), f32) * D_MODEL ** -0.5,
        "final_norm_g": 1.0 + 0.01 * nrm(ks[17], (D_MODEL,), f32),
    }


def reference(x_prompt, x_sample, cache_k, cache_v, norm_g, w_in, w_s, b_s, v_norm_g,
              lam_q1, lam_k1, lam_q2, lam_k2, attn_norm_g, w_pa, w_pb, w_out, final_norm_g):
    xp, xs = x_prompt, x_sample
    kp_l, vp_l, ks_l, vs_l, as_l = [], [], [], [], []
    for li in range(DEPTH):
        w = (norm_g[li], w_in[li], w_s[li], b_s[li], v_norm_g[li], lam_q1[li], lam_k1[li],
             lam_q2[li], lam_k2[li], attn_norm_g[li], w_pa[li], w_pb[li], w_out[li])
        xp, kp, vp, _ = mixer_layer(xp, li, *w)
        xs, ks_, vs_, as_ = mixer_layer(xs, li, *w, cache_k=cache_k[li], cache_v=cache_v[li])
        kp_l.append(kp); vp_l.append(vp)
        ks_l.append(ks_); vs_l.append(vs_); as_l.append(as_)
    y_prompt = rmsnorm(xp, final_norm_g)
    y_sample = rmsnorm(xs, final_norm_g)
    new_k_prompt = jnp.stack(kp_l, axis=0)
    new_v_prompt = jnp.stack(vp_l, axis=0)
    new_k_sample = jnp.stack(ks_l, axis=0)
    new_v_sample = jnp.stack(vs_l, axis=0)
    new_mlpv_sample = jnp.stack(as_l, axis=0)
    return (y_prompt, y_sample, new_k_prompt, new_v_prompt, new_k_sample, new_v_sample, new_mlpv_sample)
```

```python
import contextlib
import os
import math
import numpy as np
import concourse.bass as bass
import concourse.mybir as mybir
from concourse.bass_utils import run_bass_kernel_spmd

F32 = mybir.dt.float32
BF16 = mybir.dt.bfloat16
I32 = mybir.dt.int32
AF = mybir.ActivationFunctionType
ALU = mybir.AluOpType

D = 1024
H = 8
NCOL = 9216
EPS = 1e-6
KGELU = 0.7978845608028654
ENGS = ("pe", "act", "dve", "pool", "sp")


class _Stop(Exception):
    pass


def ckpt(name):
    if os.environ.get("KSTOP") == name:
        raise _Stop()


class Buf:
    __slots__ = ("w", "r", "name")

    def __init__(self, name=""):
        self.w = []
        self.r = []
        self.name = name


class Sched:
    def __init__(self, nc):
        self.nc = nc
        self.ops = {e: [] for e in ENGS}
        self.cnt = {}
        self.seen = {e: {} for e in ENGS}
        self.semnames = []
        for e in ENGS:
            self._sem("E_" + e)

    def _sem(self, name):
        if name not in self.cnt:
            self.cnt[name] = 0
            self.semnames.append(name)
        return name

    def op(self, eng, name, reads=(), writes=(), inc=True, dma=None, adds=(), **kw):
        fn = (name, kw)
        toks = []
        for b in reads:
            toks.extend(b.w)
        for b in writes:
            toks.extend(b.w)
            toks.extend(b.r)
        need = {}
        for t in toks:
            s, v = t
            if eng == "pe" and s == "E_pe":
                continue
            if self.seen[eng].get(s, 0) >= v:
                continue
            if need.get(s, 0) < v:
                need[s] = v
        for s, v in need.items():
            self.seen[eng][s] = v
        tok = None
        incspec = None
        if dma is not None:
            s = self._sem("D_" + dma)
            self.cnt[s] += 16
            tok = (s, self.cnt[s])
            incspec = (s, 16)
        elif inc:
            s = "E_" + eng
            self.cnt[s] += 1
            tok = (s, self.cnt[s])
            incspec = (s, 1)
        self.ops[eng].append((fn, list(need.items()), incspec))
        if tok is not None:
            for b in reads:
                b.r.append(tok)
                if len(b.r) > 64:
                    b.r = _compact(b.r)
            for b in writes:
                b.w = [tok]
                b.r = []
            for b in adds:
                b.w.append(tok)
                if len(b.w) > 64:
                    b.w = _compact(b.w)
        return tok

    def emit(self):
        nc = self.nc
        with contextlib.ExitStack() as st:
            sems = {}
            for n in self.semnames:
                sems[n] = st.enter_context(nc.semaphore(n))
            block = st.enter_context(nc.Block())
            engmap = {"pe": block.tensor, "act": block.scalar, "dve": block.vector,
                      "pool": block.gpsimd, "sp": block.sync}
            finals = [(n, self.cnt[n]) for n in self.semnames if self.cnt[n] > 0]

            def make(engname):
                oplist = self.ops[engname]

                def body(eng):
                    for fn, waits, incspec in oplist:
                        for s, v in waits:
                            eng.wait_ge(sems[s], v)
                        ins = getattr(eng, fn[0])(**fn[1])
                        if incspec is not None:
                            ins.then_inc(sems[incspec[0]], incspec[1])
                    if engname == "sp":
                        for s, v in finals:
                            eng.wait_ge(sems[s], v)
                return body

            for e in ENGS:
                engmap[e](make(e))


def _compact(toks):
    m = {}
    for s, v in toks:
        if m.get(s, 0) < v:
            m[s] = v
    return list(m.items())


def _unit_src(u):
    j = u % 2
    k = u // 2
    tab = [("w_in", 1024), ("w_in", 0), ("w_in", 2048), ("w_in", 7168), ("w_pa", 0),
           ("w_in", 4096), ("w_in", 5120), ("w_in", 3072), ("w_in", 6144), ("w_in", 8192),
           ("w_pb", 0), ("w_out", 0)]
    n, c = tab[k]
    return n, c + 512 * j


def build(S, P):
    NB = S // 512
    NTP = P // 128
    nc = bass.Bass("TRN2", target_bir_lowering=False)

    def din(name, shape, dt=F32):
        return nc.dram_tensor(name, shape, dt, kind="ExternalInput").ap()

    def dout(name, shape):
        return nc.dram_tensor(name, shape, F32, kind="ExternalOutput").ap()

    def dscr(name, shape, dt=BF16):
        return nc.dram_tensor(name, shape, dt, kind="Internal").ap()

    xp = din("xp", [S, D])
    xs = din("xs", [128, D])
    ck = din("ck", [2, 2, P, D])
    cv = din("cv", [2, 2, P, D])
    norm_g = din("norm_g", [2, D])
    wts = {"w_in": din("w_in", [2, D, NCOL]), "w_pa": din("w_pa", [2, D, D]),
           "w_pb": din("w_pb", [2, D, D]), "w_out": din("w_out", [2, D, D])}
    w_s = din("w_s", [2, 4, 128, 128])
    b_s = din("b_s", [2, 4, 128])
    v_norm_g = din("v_norm_g", [2, D])
    lamv = [din(n, [2, 64]) for n in ("lam_q1", "lam_k1", "lam_q2", "lam_k2")]
    attn_g = din("attn_norm_g", [2, 128])
    final_g = din("final_norm_g", [D])
    ident = din("ident", [128, 128])
    tril = din("tril", [2, 128, 128])

    yp = dout("yp", [S, D])
    ys = dout("ys", [128, D])
    nkp = dout("nkp", [2, S, D])
    nvp = dout("nvp", [2, S, D])
    nks = dout("nks", [2, 128, D])
    nvs = dout("nvs", [2, 128, D])
    nms = dout("nms", [2, 128, D])

    WS = dscr("WS", [2, 24, 128, 8, 512])
    KTS = dscr("KTS", [2, H, 128, S])
    VSS = dscr("VSS", [2, H, 128, S // 128, 129])
    KTC = dscr("KTC", [2, 2, H, 128, P])
    VC = dscr("VC", [2, 2, H, 128, NTP, 129])

    S_ = Sched(nc)
    op = S_.op
    NHMAX = max((NB - 1) * 4, NTP, 1)

    with contextlib.ExitStack() as st:
        def sb(name, shape, dt):
            return st.enter_context(nc.sbuf_tensor(name, shape, dt))

        X = sb("X", [128, 4, D], F32)
        HT = sb("HT", [128, 8, 512], BF16)
        YB = HT[:].rearrange("p a b -> p (a b)").rearrange("p (t c) -> p t c", t=4)
        UT = sb("UT", [128, 8, 512], BF16)
        YBT = UT
        VN = sb("VN", [128, 4, D], BF16)
        KT = VN[:].rearrange("p a b -> p (a b)").rearrange("p (h c) -> p h c", h=8)
        MBT = KT
        SG = sb("SG", [128, 8, 512], BF16)
        MT = sb("MT", [128, 8, 512], BF16)
        QT = UT
        VA = sb("VA", [128, 8, 4, 129], BF16)
        GZ = sb("GZ", [128, 4, D], BF16)
        KH = [sb("KH%d" % i, [128, NHMAX * 128], BF16) for i in range(2)]
        VH = [sb("VH%d" % i, [128, NHMAX, 129], BF16) for i in range(2)]
        NPT = 4
        PT = [sb("PT%d" % i, [128, 2, 512], BF16) for i in range(NPT)]
        PD = [sb("PD%d" % i, [128, 2, 512], BF16) for i in range(4)]
        NW = 4
        W = [sb("W%d" % i, [128, 8, 512], BF16) for i in range(NW)]
        STG = [sb("STG%d" % i, [128, D], F32) for i in range(2)]
        TA = [sb("TA%d" % i, [128, D], F32) for i in range(2)]
        TBF = [sb("TBF%d" % i, [128, D], F32) for i in range(2)]
        SB16 = [sb("SB16_%d" % i, [128, D], BF16) for i in range(2)]
        OS = [sb("OS%d" % i, [128, 2, 4, 129], F32) for i in range(2)]
        OA1 = sb("OA", [128, 4, 128], F32)
        GVB = sb("GVB", [128, D], F32)
        GV = [GVB, GVB]
        FG = GVB
        AG = [sb("AG%d" % i, [128, 128], F32) for i in range(2)]
        NG = sb("NG", [128, 2, 8], F32)
        LQ = TBF[0][:, 0:512].rearrange("p (a b) -> p a b", a=4)
        LS = sb("LS", [128, 8], F32)
        NLAM = sb("NLAM", [128, 2], F32)
        IDF = TA[1][:, 0:128]
        IDB = sb("IDB", [128, 128], BF16)
        TRL = sb("TRL", [128, 2, 128], F32)
        WSF = TA[1][:, 128:256]
        WST = sb("WST", [128, 2, 2, 4, 128], BF16)
        BFT = TA[0][0:33, :].rearrange("p (l g t) -> p l g t", l=2, g=4)
        BHI = TBF[1][0:33, 0:512].bitcast(BF16).rearrange("p (l g t) -> p l g t", l=2, g=4)
        BLO = TBF[1][0:33, 512:1024].bitcast(BF16).rearrange("p (l g t) -> p l g t", l=2, g=4)
        BR = sb("BR", [33, 2, 2, 4, 128], BF16)
        ONES = sb("ONES", [33, 128], BF16)
        SSQ = sb("SSQ", [128, 4], F32)
        RSQ = sb("RSQ", [128, 4], F32)
        SSV = sb("SSV", [128, 4], F32)
        RSV = sb("RSV", [128, 4], F32)
        SSO = sb("SSO", [128, 8, 4], F32)
        RSO = sb("RSO", [128, 8, 4], F32)
        RL = sb("RL", [128, 2, 4], F32)
        RQF = sb("RQF", [128, 3, 32], F32)
        RQI = sb("RQI", [128, 32], I32)
        RQY = sb("RQY", [128, 32], F32)
        PS = st.enter_context(nc.psum_tensor("PS", [128, 8 * 512], F32))
        PSB = PS[:].bitcast(BF16)

        bX = [Buf() for _ in range(4)]
        bHT = [Buf() for _ in range(8)]
        bYB = [Buf() for _ in range(4)]
        bUT = [Buf() for _ in range(8)]
        bVN = [Buf() for _ in range(4)]
        bKT = Buf()
        bMBT = [Buf() for _ in range(8)]
        bSG = [Buf() for _ in range(8)]
        bMT = [Buf() for _ in range(8)]
        bQT = [Buf() for _ in range(8)]
        bVA = [Buf() for _ in range(4)]
        bGZ = [Buf() for _ in range(4)]
        bKH = [Buf() for _ in range(2)]
        bVH = [Buf() for _ in range(2)]
        bPT = [Buf() for _ in range(NPT)]
        bPD = [Buf() for _ in range(4)]
        bW = [Buf() for _ in range(NW)]
        bSTG = [Buf() for _ in range(2)]
        bTA = [Buf() for _ in range(2)]
        bTB = [Buf() for _ in range(2)]
        bSB16 = [Buf() for _ in range(2)]
        bOS = [Buf() for _ in range(2)]
        bOA = Buf()
        bJ = Buf()
        bGV = Buf()
        bBK = [Buf() for _ in range(8)]
        bC = Buf()
        bSSQ = Buf(); bSSV = Buf(); bSSO = Buf(); bRL = Buf(); bRQ = Buf()
        bSSOh = [Buf() for _ in range(H)]
        bWS = {}
        bKTS = {}
        bVSS = {}
        bKTC = {}
        bVC = {}

        st_cnt = {"bank": 0, "pair": 0, "i2": 0, "pt": 0, "hs": 0, "wn": 0, "ev": 0}

        def bank():
            b = st_cnt["bank"] % 8
            st_cnt["bank"] += 1
            return b

        def pair():
            b = (st_cnt["pair"] % 4) * 2
            st_cnt["pair"] += 1
            return b

        def bk(b, n=512, c0=0):
            return PS[:, b * 512 + c0:b * 512 + c0 + n]

        op("sp", "dma_start", out=IDF[:], in_=ident, writes=[bC], dma="c0")
        op("sp", "dma_start", out=TRL[:], in_=tril.rearrange("v t s -> t v s"), writes=[bC], dma="c0")
        for L in range(2):
            op("sp", "dma_start", out=NG[:, L, :], in_=norm_g[L].rearrange("(kc p) -> p kc", p=128), allow_slow_non_contiguous=True, writes=[bC], dma="c0")
            op("sp", "dma_start", out=AG[L][:], in_=attn_g[L].partition_broadcast(128), writes=[bC], dma="c0")
        for i in range(4):
            op("sp", "dma_start", out=LQ[:, i, :], in_=lamv[i].rearrange("l d -> (l d)").partition_broadcast(128),
               writes=[bC], dma="c0")
        op("dve", "memset", ap=BFT[:], constant=0.0, writes=[bC])
        op("sp", "dma_start", out=BFT[0:1], in_=b_s.rearrange("(o l) g t -> o l g t", o=1), writes=[bC], dma="c0")
        op("sp", "dma_start", out=BFT[32:33], in_=b_s.rearrange("(o l) g t -> o l g t", o=1), writes=[bC], dma="c0")
        op("dve", "tensor_copy", out=IDB[:], in_=IDF[:], reads=[bC], writes=[bC])
        for L in range(2):
            lam_init = 0.8 - 0.6 * math.exp(-0.3 * L)
            op("dve", "tensor_scalar", out=AG[L][:], in0=AG[L][:], scalar1=0.5 * (1.0 - lam_init), scalar2=None, op0=ALU.mult, reads=[bC], writes=[bC])
        for w_ in range(2):
            op("dve", "tensor_tensor", out=LQ[:, 2 * w_, :], in0=LQ[:, 2 * w_, :], in1=LQ[:, 2 * w_ + 1, :], op=ALU.mult, reads=[bC], writes=[bC])
            op("dve", "reduce_sum", out=LS[:, 2 * w_:2 * w_ + 2], in_=LQ[:, 2 * w_, :].rearrange("p (l d) -> p l d", l=2), axis=mybir.AxisListType.X, reads=[bC], writes=[bC])
        op("act", "activation", out=LS[:, 4:8], in_=LS[:, 0:4], func=AF.Exp, reads=[bC], writes=[bC])
        op("dve", "tensor_tensor", out=NLAM[:], in0=LS[:, 6:8], in1=LS[:, 4:6], op=ALU.subtract, reads=[bC], writes=[bC])
        for L in range(2):
            lam_init = 0.8 - 0.6 * math.exp(-0.3 * L)
            op("dve", "tensor_scalar", out=NLAM[:, L:L + 1], in0=NLAM[:, L:L + 1], scalar1=-lam_init, scalar2=None, op0=ALU.add, reads=[bC], writes=[bC])
        op("dve", "memset", ap=ONES[:], constant=0.0, writes=[bC])
        op("dve", "memset", ap=ONES[0:1], constant=1.0, writes=[bC])
        op("dve", "memset", ap=ONES[32:33], constant=1.0, writes=[bC])
        op("dve", "memset", ap=BR[:], constant=0.0, writes=[bC])
        op("dve", "tensor_copy", out=BHI[:], in_=BFT[:], reads=[bC], writes=[bC])
        op("dve", "tensor_tensor", out=BLO[:], in0=BFT[:], in1=BHI[:], op=ALU.subtract, reads=[bC], writes=[bC])
        op("dve", "tensor_copy", out=BR[0:1, 0], in_=BHI[0:1], reads=[bC], writes=[bC])
        op("dve", "tensor_copy", out=BR[32:33, 0], in_=BLO[32:33], reads=[bC], writes=[bC])
        for hf in range(2):
            op("dve", "tensor_copy", out=BR[0:1, 1, :, :, hf * 64:(hf + 1) * 64], in_=BHI[0:1, :, :, 0:64],
               reads=[bC], writes=[bC])
            op("dve", "tensor_copy", out=BR[32:33, 1, :, :, hf * 64:(hf + 1) * 64], in_=BLO[32:33, :, :, 0:64],
               reads=[bC], writes=[bC])
        op("dve", "memset", ap=VA[:, :, :, 128:129], constant=1.0, writes=bVA)
        for i in range(2):
            op("dve", "memset", ap=VH[i][:, :, 128:129], constant=1.0, writes=[bVH[i]])
        for j in range(4):
            op("dve", "memset", ap=PD[j][:], constant=0.0, writes=[bPD[j]])
        for j in range(NPT):
            op("dve", "memset", ap=PT[j][:], constant=0.0, writes=[bPT[j]])
        for var in range(2):
            for L in range(2):
                for g in range(4):
                    if var == 0:
                        op("sp", "dma_start", out=WSF[:], in_=w_s[L, g], writes=[bTA[0]], dma="c1")
                    else:
                        op("dve", "memset", ap=WSF[:], constant=0.0, writes=[bTA[0]])
                        op("sp", "dma_start", out=WSF[0:64, 0:64], in_=w_s[L, g, 0:64, 0:64],
                           writes=[bTA[0]], dma="c1")
                        op("sp", "dma_start", out=WSF[64:128, 64:128], in_=w_s[L, g, 0:64, 0:64],
                           reads=[bTA[0]], dma="c1")
                        bTA[0].w = bTA[0].w + bTA[0].r
                        bTA[0].r = []
                    op("dve", "tensor_tensor", out=SB16[0][:, 0:128], in0=WSF[:], in1=TRL[:, var, :], op=ALU.mult,
                       reads=[bTA[0], bC], writes=[bSB16[0]])
                    b = bank()
                    op("pe", "transpose", out=PSB[:, b * 1024:b * 1024 + 128], in_=SB16[0][:, 0:128], identity=IDB[:],
                       reads=[bSB16[0], bC], writes=[bBK[b]])
                    op("dve", "tensor_copy", out=WST[:, var, L, g, :], in_=PSB[:, b * 1024:b * 1024 + 128],
                       reads=[bBK[b]], writes=[bC])

        wstate = {"issued": 0}

        def wissue(n, first_block):
            L = (n % 48) // 24
            u = n % 24
            slot = n % NW
            if n < 48:
                name, c0 = _unit_src(u)
                src = wts[name][L][:, c0:c0 + 512].rearrange("(kc p) n -> p kc n", p=128)
                op("pool", "dma_start", out=W[slot][:], in_=src, writes=[bW[slot]], dma="wp%d" % slot)
                bWS[(L, u)] = Buf()
                op("sp", "dma_start", out=WS[L, u], in_=W[slot][:], reads=[bW[slot]], writes=[bWS[(L, u)]],
                   dma="ws%d" % slot)
            else:
                op("sp", "dma_start", out=W[slot][:], in_=WS[L, u], reads=[bWS[(L, u)]], writes=[bW[slot]],
                   dma="w%d" % slot)

        def wneed(n, nmax, first_block, live=None):
            live = n if live is None else live
            while wstate["issued"] <= min(live + NW - 1, nmax):
                wissue(wstate["issued"], first_block)
                wstate["issued"] += 1
            return n % NW

        def prepass_v(L, s, h):
            sl = st_cnt["hs"] % 2
            st_cnt["hs"] += 1
            src = cv[L, s].rearrange("(t p) (h e) -> p h t e", p=128, h=H)[:, h]
            op("pool", "dma_start", out=VH[sl][:, 0:NTP, 0:128], in_=src, writes=[bVH[sl]], dma="vhp%d" % sl)
            bVC[(L, s, h)] = Buf()
            op("sp", "dma_start", out=VC[L, s, h], in_=VH[sl][:, 0:NTP, :], reads=[bVH[sl]], writes=[bVC[(L, s, h)]],
               dma="vc%d" % sl)

        def prepass_k(L, s, t0, ntl):
            pbufs = [(TA[0][:].bitcast(BF16)[:, 0:D], bTA[0]), (TA[1][:].bitcast(BF16)[:, 0:D], bTA[1]),
                     (TBF[0][:].bitcast(BF16)[:, 0:D], bTB[0]), (TBF[1][:].bitcast(BF16)[:, 0:D], bTB[1])]
            for tt in range(ntl):
                pb_, pbb = pbufs[tt]
                src = ck[L, s, (t0 + tt) * 128:(t0 + tt + 1) * 128, :]
                op("pool", "dma_start", out=pb_, in_=src, writes=[pbb], dma="pk%d" % tt)
            for tt in range(ntl):
                pb_, pbb = pbufs[tt]
                b = bank()
                for h in range(H):
                    op("pe", "transpose", out=PSB[:, b * 1024 + h * 128:b * 1024 + (h + 1) * 128], in_=pb_[:, h * 128:(h + 1) * 128], identity=IDB[:],
                       reads=[pbb, bC], writes=[bBK[b]], inc=True)
                src_ps = PSB[:, b * 1024:(b + 1) * 1024].rearrange("p (h c) -> p h c", h=H)
                if tt % 2 == 0:
                    op("act", "activation", out=KT[:, :, tt * 128:(tt + 1) * 128], in_=src_ps, func=AF.Copy,
                       reads=[bBK[b]], writes=[bKT] + bVN + bMBT)
                else:
                    op("dve", "tensor_copy", out=KT[:, :, tt * 128:(tt + 1) * 128], in_=src_ps,
                       reads=[bBK[b]], writes=[bKT] + bVN + bMBT)
            key = (L, s, t0)
            bKTC[key] = Buf()
            op("sp", "dma_start", out=KTC[L, s, :, :, t0 * 128:(t0 + ntl) * 128].rearrange("h p c -> p h c"), in_=KT[:, :, 0:ntl * 128], reads=[bKT], writes=[bKTC[key]], dma="ktc")

        pre_items = []
        for L in range(2):
            for s in range(2):
                for h in range(H):
                    pre_items.append(("v", L, s, h))
                for t0 in range(0, NTP, 4):
                    pre_items.append(("k", L, s, t0, min(4, NTP - t0)))

        def run_pre(items):
            for it in items:
                if it[0] == "v":
                    prepass_v(*it[1:])
                else:
                    prepass_k(*it[1:])

        def rsqrt(src, dst, n, scale, eps, rbufs, wbufs):
            shp = list(src.shape)
            tot = 1
            for d_ in shp[1:]:
                tot *= d_

            def v(ap):
                if len(shp) == 3:
                    return ap[:, 0:tot].rearrange("p (a b) -> p a b", a=shp[1])
                return ap[:, 0:tot]
            t = v(RQF[:, 0, :]); a = v(RQF[:, 1, :]); c = v(RQF[:, 2, :])
            yi = v(RQY[:].bitcast(I32))
            y = v(RQY[:, :])
            ti = v(RQF[:, 0, :].bitcast(I32))
            op("dve", "tensor_scalar", out=t, in0=src, scalar1=scale, scalar2=eps, op0=ALU.mult, op1=ALU.add,
               reads=rbufs, writes=[bRQ])
            op("dve", "tensor_copy", out=a, in_=ti, writes=[bRQ])
            op("dve", "tensor_scalar", out=a, in0=a, scalar1=-0.5, scalar2=float(0x5f3759df), op0=ALU.mult, op1=ALU.add, writes=[bRQ])
            op("dve", "tensor_copy", out=yi, in_=a, writes=[bRQ])
            for it in range(3):
                op("dve", "tensor_tensor", out=a, in0=y, in1=y, op=ALU.mult, writes=[bRQ])
                op("dve", "tensor_tensor", out=c, in0=a, in1=t, op=ALU.mult, writes=[bRQ])
                op("dve", "tensor_scalar", out=c, in0=c, scalar1=-0.5, scalar2=1.5, op0=ALU.mult, op1=ALU.add,
                   writes=[bRQ])
                if it < 2:
                    op("dve", "tensor_tensor", out=y, in0=y, in1=c, op=ALU.mult, writes=[bRQ])
                else:
                    op("dve", "tensor_tensor", out=dst, in0=y, in1=c, op=ALU.mult, reads=[bRQ], writes=wbufs)

        def hist_load_inst(L, h, g):
            nh = g["nh"]
            if nh == 0:
                return None
            sl = st_cnt["hs"] % 2
            st_cnt["hs"] += 1
            ksrc, kb, vsrc, vb = g["hist"](L, h)
            op("sp", "dma_start", out=KH[sl][:, 0:nh * 128], in_=ksrc, reads=kb, writes=[bKH[sl]], dma="kh%d" % sl)
            op("sp", "dma_start", out=VH[sl][:, 0:nh, :], in_=vsrc, reads=vb, writes=[bVH[sl]], dma="vh%d" % sl)
            return sl

        def attention(L, blk):
            nt = blk["nt"]
            N = nt * 128
            var = blk["var"]
            insts = []
            for h in range(H):
                for g in blk["groups"]:
                    insts.append((h, g))

            def hist_load(idx):
                return hist_load_inst(L, *insts[idx])

            def _unused(idx):
                h, g = insts[idx]
                nh = g["nh"]
                if nh == 0:
                    return None
                sl = st_cnt["hs"] % 2
                st_cnt["hs"] += 1
                ksrc, kb, vsrc, vb = g["hist"](L, h)
                op("sp", "dma_start", out=KH[sl][:, 0:nh * 128], in_=ksrc, reads=kb, writes=[bKH[sl]], dma="kh%d" % sl)
                op("sp", "dma_start", out=VH[sl][:, 0:nh, :], in_=vsrc, reads=vb, writes=[bVH[sl]], dma="vh%d" % sl)
                return sl

            OACC = PS[:, 2048:4096].rearrange("p (m t c) -> p m t c", m=2, t=4, c=256)
            slots = {0: blk.pop("hist0") if "hist0" in blk else hist_load(0)}
            fin_i = 0
            pending = []
            for idx, (h, g) in enumerate(insts):
                if idx + 1 < len(insts):
                    slots[idx + 1] = hist_load(idx + 1)
                sl = slots[idx]
                nh = g["nh"]
                qc0, nq, qp0, nqr = g["qc0"], g["nq"], g["qp0"], g["nqr"]
                kts = []
                for i in range(nh):
                    kts.append(dict(kind="hist", K=lambda m, i=i: KH[sl][m * 64:(m + 1) * 64, i * 128:(i + 1) * 128],
                                    V=VH[sl][:, i, :], kp0=0, nk=128, c0=qc0, c1=qc0 + nq, rb=[bKH[sl]], rvb=[bVH[sl]],
                                    qts=list(range(g["nqt"]))))
                if g["causal"]:
                    for j in range(nt):
                        kts.append(dict(kind="diag", j=j, K=lambda m, j=j: KT[m * 64:(m + 1) * 64, h, j * 128:(j + 1) * 128],
                                        V=VA[:, h, j, :], kp0=0, nk=128, c0=j * 128, c1=N, rb=[bKT], rvb=[bVA[j]],
                                        qts=list(range(j, nt))))
                else:
                    k0, nk = g["cur"]
                    kts.append(dict(kind="cur", K=lambda m, k0=k0, nk=nk: KT[m * 64:(m + 1) * 64, h, k0:k0 + nk],
                                    V=VA[k0:k0 + nk, h, 0, :], kp0=k0, nk=nk, c0=qc0, c1=qc0 + nq, rb=[bKT], rvb=[bVA[0]],
                                    qts=[0]))
                started = set()

                def qk(kt, sbk):
                    STB = PS[:, sbk * 512:sbk * 512 + 1024].rearrange("p (m c) -> p m c", m=2)
                    for m in range(2):
                        op("pe", "matmul", out=STB[kt["kp0"]:kt["kp0"] + kt["nk"], m, kt["c0"]:kt["c1"]], lhsT=kt["K"](m), rhs=QT[m * 64:(m + 1) * 64, h, kt["c0"]:kt["c1"]], start=True, stop=True,
                           reads=kt["rb"] + [bQT[h]], writes=[bBK[sbk + m]])
                    return STB

                def ex(kt, sbk, STB):
                    r0, r1 = kt["kp0"], kt["kp0"] + kt["nk"]
                    if kt["kind"] == "diag":
                        j = kt["j"]
                        Pt, pb = PD[j], bPD[j]
                        pieces = []
                        if (j + 1) * 128 < N:
                            pieces.append((0, 128, (j + 1) * 128, N))
                        pieces.append((0, 64, j * 128, (j + 1) * 128))
                        pieces.append((64, 128, j * 128 + 64, (j + 1) * 128))
                    else:
                        pi = st_cnt["pt"] % NPT
                        st_cnt["pt"] += 1
                        Pt, pb = PT[pi], bPT[pi]
                        pieces = [(r0, r1, kt["c0"], kt["c1"])]
                    for pi_, (a0, a1, c0, c1) in enumerate(pieces):
                        op("act", "activation", out=Pt[a0:a1, :, c0:c1], in_=STB[a0:a1, :, c0:c1], func=AF.Exp, scale=0.125,
                           reads=[bBK[sbk], bBK[sbk + 1]], writes=[pb] if pi_ == 0 else [], adds=[] if pi_ == 0 else [pb])
                    return Pt, pb

                def pv(kt, Pt, pb, last):
                    for qt in kt["qts"]:
                        for m in range(2):
                            bankid = 4 + m * 2 + qt // 2
                            key = (bankid, qp0)
                            stt = key not in started
                            started.add(key)
                            if g["causal"]:
                                o_ap = OACC[0:128, m, qt, 0:129]
                                l_ap = Pt[kt["kp0"]:kt["kp0"] + kt["nk"], m, qt * 128:(qt + 1) * 128]
                            else:
                                o_ap = OACC[0:128, m, g["oslot"], 0:129]
                                l_ap = Pt[kt["kp0"]:kt["kp0"] + kt["nk"], m, 0:128]
                            op("pe", "matmul", out=o_ap, lhsT=l_ap, rhs=kt["V"], start=stt, stop=last, skip_group_check=True,
                               reads=[pb] + kt["rvb"], writes=[bBK[bankid]] if stt else [], inc=True)
                            if not stt:
                                bBK[bankid].w = [("E_pe", S_.cnt["E_pe"])]

                sbks = [0, 2]
                cur = qk(kts[0], sbks[0])
                for i, kt in enumerate(kts):
                    nxt = None
                    if i + 1 < len(kts):
                        nxt = qk(kts[i + 1], sbks[(i + 1) % 2])
                    Pt, pb = ex(kt, sbks[i % 2], cur)
                    pv(kt, Pt, pb, i == len(kts) - 1)
                    cur = nxt
                    if pending and (i == 2 or i == len(kts) - 1):
                        pending.pop(0)()
                if g is blk["groups"][-1]:
                    fi = fin_i % 2
                    fin_i += 1
                    obk = [bBK[4], bBK[5], bBK[6], bBK[7]]
                    if var == 0:
                        op("dve", "tensor_copy", out=OS[fi][:, :, 0:nt, :], in_=OACC[:, :, 0:nt, 0:129], reads=obk, writes=[bOS[fi]])
                    else:
                        op("dve", "tensor_copy", out=OS[fi][0:64, :, 0, :], in_=OACC[0:64, :, 0, 0:129], reads=obk, writes=[bOS[fi]])
                        op("dve", "tensor_copy", out=OS[fi][64:128, :, 0, :], in_=OACC[64:128, :, 1, 0:129], reads=obk, adds=[bOS[fi]])

                    def fin_rest(h=h, fi=fi):
                        op("dve", "reciprocal", out=RL[:, :, 0:nt], in_=OS[fi][:, :, 0:nt, 128], reads=[bOS[fi]], writes=[bRL])
                        op("dve", "tensor_tensor", out=OS[fi][:, :, 0:nt, 0:128], in0=OS[fi][:, :, 0:nt, 0:128],
                           in1=RL[:, :, 0:nt].unsqueeze(3).to_broadcast([128, 2, nt, 128]), op=ALU.mult, reads=[bRL], writes=[bOS[fi]])
                        op("dve", "scalar_tensor_tensor", out=OA1[:, 0:nt, :], in0=OS[fi][:, 1, 0:nt, 0:128], scalar=NLAM[:, L:L + 1],
                           in1=OS[fi][:, 0, 0:nt, 0:128], op0=ALU.mult, op1=ALU.add, reads=[bOS[fi], bC], writes=[bOA])
                        for qt in range(nt):
                            op("act", "activation", out=OS[fi][:, 1, qt, 0:128], in_=OA1[:, qt, :], func=AF.Square, accum_out=SSO[:, h, qt:qt + 1],
                               reads=[bOA], writes=[bSSOh[h], bOS[fi]] if qt == 0 else [],
                               adds=[] if qt == 0 else [bSSOh[h], bOS[fi]])
                        rsqrt(SSO[:, h, 0:nt], RSO[:, h, 0:nt], None, 1.0 / 128, EPS, [bSSOh[h]], [bSSOh[h]])
                        for qt in range(nt):
                            first = (h == 0 and qt == 0)
                            op("dve", "scalar_tensor_tensor", out=YB[:, qt, h * 128:(h + 1) * 128], in0=OA1[:, qt, :], scalar=RSO[:, h, qt:qt + 1],
                               in1=GZ[:, qt, h * 128:(h + 1) * 128], op0=ALU.mult, op1=ALU.mult,
                               reads=[bOA, bSSOh[h], bGZ[qt]], writes=(bYB[0:nt] + bHT) if first else [], adds=[] if first else [bYB[qt]])
                    pending.append(fin_rest)
            while pending:
                pending.pop(0)()


        def fm_unit_safe(slot, j, rhs_of_kc, rbufs, N):
            b = bank()
            for kc in range(8):
                rb = [bW[slot]] + rbufs(kc)
                if kc == 7:
                    for k2 in range(7):
                        rb = rb + rbufs(k2)
                op("pe", "matmul", out=bk(b, N), lhsT=W[slot][:, kc, j * 128:(j + 1) * 128], rhs=rhs_of_kc(kc), start=(kc == 0), stop=(kc == 7),
                   reads=rb, writes=[bBK[b]], inc=(kc == 7 or kc == 0))
            return b

        def tm_pair(slots, t, lhs_of_kc, lbufs):
            b = pair()
            for u2 in range(2):
                for kc in range(8):
                    rb = [bW[slots[u2]]] + lbufs(kc, t)
                    if kc == 7:
                        for k2 in range(7):
                            rb = rb + lbufs(k2, t)
                    op("pe", "matmul", out=bk(b + u2), lhsT=lhs_of_kc(kc, t), rhs=W[slots[u2]][:, kc, :], start=(kc == 0), stop=(kc == 7),
                       reads=rb, writes=[bBK[b + u2]], inc=(kc == 7 or kc == 0))
            return b

        def do_block(blk, bi, nblocks_total):
            nt = blk["nt"]
            N = nt * 128
            var = blk["var"]
            first = (bi == 0)
            nmax = nblocks_total * 48 - 1
            base = bi * 48
            for t in range(nt):
                op("sp", "dma_start", out=X[:, t, :], in_=blk["x"][:, t, :], writes=[bX[t]], dma="x%d" % t)
            for L in range(2):
                ub = base + L * 24
                op("sp", "dma_start", out=GVB[:], in_=v_norm_g[L].partition_broadcast(128), writes=[bGV], dma="gv")
                for t in range(nt):
                    i = st_cnt["i2"] % 2
                    st_cnt["i2"] += 1
                    op("act", "activation", out=SB16[i][:], in_=X[:, t, :], func=AF.Square, accum_out=SSQ[:, t:t + 1],
                       reads=[bX[t]], writes=[bSB16[i]] + ([bSSQ] if t == 0 else []), adds=[] if t == 0 else [bSSQ])
                rsqrt(SSQ[:, 0:nt], RSQ[:, 0:nt], None, 1.0 / D, EPS, [bSSQ], [bSSQ])
                run_pre(blk["pre"] if L == 0 else blk["pre_b"])
                tb = [bank() for _ in range(4)]
                for t in range(nt):
                    i = st_cnt["i2"] % 2
                    st_cnt["i2"] += 1
                    op("dve", "tensor_scalar", out=SB16[i][:], in0=X[:, t, :], scalar1=RSQ[:, t:t + 1], scalar2=None, op0=ALU.mult, reads=[bX[t], bSSQ], writes=[bSB16[i]])
                    for kc in range(8):
                        b = tb[kc // 2]
                        c0 = b * 1024 + (kc % 2) * 512 + t * 128
                        op("pe", "transpose", out=PSB[:, c0:c0 + 128], in_=SB16[i][:, kc * 128:(kc + 1) * 128], identity=IDB[:],
                           reads=[bSB16[i], bC], writes=[bBK[b]], inc=True)
                for kc in range(8):
                    b = tb[kc // 2]
                    c0 = b * 1024 + (kc % 2) * 512
                    if kc % 2 == 0:
                        op("act", "activation", out=HT[:, kc, 0:N], in_=PSB[:, c0:c0 + N], func=AF.Copy, scale=NG[:, L, kc:kc + 1],
                           reads=[bBK[b], bC], writes=[bHT[kc]] + bYB)
                    else:
                        op("dve", "tensor_scalar", out=HT[:, kc, 0:N], in0=PSB[:, c0:c0 + N], scalar1=NG[:, L, kc:kc + 1], scalar2=None, op0=ALU.mult,
                           reads=[bBK[b], bC], writes=[bHT[kc]] + bYB)

                ckpt("ht%d" % L)

                def rhsHT(kc):
                    return HT[:, kc, 0:N]

                def rbHT(kc):
                    return [bHT[kc]]

                def lhsHT(kc, t):
                    return HT[:, kc, t * 128:(t + 1) * 128]

                def lbHT(kc, t):
                    return [bHT[kc]]

                def gelu_chain(src, n, i, rbk):
                    op("act", "activation", out=TA[i][:, 0:n], in_=src, func=AF.Square, reads=rbk, writes=[bTA[i]])
                    op("dve", "scalar_tensor_tensor", out=TA[i][:, 0:n], in0=TA[i][:, 0:n], scalar=1.0 / 0.044715, in1=src,
                       op0=ALU.add, op1=ALU.mult, reads=rbk, writes=[bTA[i]])
                    op("act", "activation", out=TBF[i][:, 0:n], in_=TA[i][:, 0:n], func=AF.Tanh, scale=KGELU * 0.044715,
                       reads=[bTA[i]], writes=[bTB[i]])

                ckpt("z%d" % L)
                s0 = wneed(ub + 0, nmax, first)
                s1 = wneed(ub + 1, nmax, first, live=ub + 0)
                for tp0 in range(0, nt, 2):
                    tl = list(range(tp0, min(tp0 + 2, nt)))
                    info = {}
                    for t in tl:
                        b = tm_pair((s0, s1), t, lhsHT, lbHT)
                        info[t] = (b, t % 2, PS[:, b * 512:b * 512 + 1024], [bBK[b], bBK[b + 1]])
                    for t in tl:
                        b, i, src, rbk = info[t]
                        op("act", "activation", out=TA[i][:], in_=src, func=AF.Square, reads=rbk, writes=[bTA[i]])
                    for t in tl:
                        b, i, src, rbk = info[t]
                        op("dve", "scalar_tensor_tensor", out=TA[i][:], in0=TA[i][:], scalar=1.0 / 0.044715, in1=src,
                           op0=ALU.add, op1=ALU.mult, reads=rbk, writes=[bTA[i]])
                    for t in tl:
                        b, i, src, rbk = info[t]
                        op("act", "activation", out=TBF[i][:], in_=TA[i][:], func=AF.Tanh, scale=KGELU * 0.044715,
                           reads=[bTA[i]], writes=[bTB[i]])
                    for t in tl:
                        b, i, src, rbk = info[t]
                        op("dve", "scalar_tensor_tensor", out=TA[i][:], in0=TBF[i][:], scalar=1.0, in1=src, op0=ALU.add, op1=ALU.mult,
                           reads=[bTB[i]] + rbk, writes=[bTA[i]])
                    for k2, t in enumerate(tl):
                        b, i, src, rbk = info[t]
                        op("act", "activation", out=TBF[i][:], in_=TA[i][:], func=AF.Square, accum_out=SSV[:, t:t + 1],
                           reads=[bTA[i]], writes=[bTB[i]] + ([bSSV] if k2 == 0 else []), adds=[] if k2 == 0 else [bSSV])
                    rsqrt(SSV[:, tl[0]:tl[-1] + 1], RSV[:, tl[0]:tl[-1] + 1], None, 1.0 / D, 4.0 * EPS, [bSSV], [bSSV])
                    for t in tl:
                        b, i, src, rbk = info[t]
                        if var == 0:
                            op("dve", "scalar_tensor_tensor", out=VN[:, t, :], in0=TA[i][:], scalar=RSV[:, t:t + 1], in1=GV[L][:], op0=ALU.mult, op1=ALU.mult,
                               reads=[bTA[i], bSSV, bC, bGV], writes=[bVN[t], bKT] + bMBT)
                        else:
                            k_ = st_cnt["ev"] % 2
                            st_cnt["ev"] += 1
                            op("dve", "scalar_tensor_tensor", out=STG[k_][:], in0=TA[i][:], scalar=RSV[:, t:t + 1], in1=GV[L][:], op0=ALU.mult, op1=ALU.mult,
                               reads=[bTA[i], bSSV, bC, bGV], writes=[bSTG[k_]])
                            op("sp", "dma_start", out=nms[L], in_=STG[k_][:], reads=[bSTG[k_]], dma="stg%d" % k_)
                            op("dve", "tensor_copy", out=VN[:, t, :], in_=STG[k_][:],
                               reads=[bSTG[k_]], writes=[bVN[t], bKT] + bMBT)
                for u in range(2):
                    slot = wneed(ub + 2 + u, nmax, first)
                    for j in range(4):
                        cg = u * 4 + j
                        b = fm_unit_safe(slot, j, rhsHT, rbHT, N)
                        i = st_cnt["i2"] % 2
                        st_cnt["i2"] += 1
                        gelu_chain(bk(b, N), N, i, [bBK[b]])
                        op("dve", "scalar_tensor_tensor", out=UT[:, cg, 0:N], in0=TBF[i][:, 0:N], scalar=1.0, in1=bk(b, N), op0=ALU.add, op1=ALU.mult,
                           reads=[bTB[i], bBK[b]], writes=[bUT[cg]])
                ckpt("u%d" % L)
                for u in range(2):
                    slot = wneed(ub + 4 + u, nmax, first)
                    for j in range(4):
                        cg = u * 4 + j
                        b = fm_unit_safe(slot, j, rhsHT, rbHT, N)
                        i = st_cnt["i2"] % 2
                        st_cnt["i2"] += 1
                        op("act", "activation", out=TBF[i][:, 0:N], in_=bk(b, N), func=AF.Tanh, scale=0.5,
                           reads=[bBK[b]], writes=[bTB[i]])
                        op("dve", "scalar_tensor_tensor", out=TA[i][:, 0:N], in0=TBF[i][:, 0:N], scalar=1.0, in1=bk(b, N), op0=ALU.add, op1=ALU.mult,
                           reads=[bTB[i], bBK[b]], writes=[bTA[i]])
                        op("pool", "tensor_tensor", out=UT[:, cg, 0:N], in0=UT[:, cg, 0:N], in1=TA[i][:, 0:N], op=ALU.mult, reads=[bTA[i]], writes=[bUT[cg]])
                ckpt("sp%d" % L)
                for u in range(2):
                    slot = wneed(ub + 6 + u, nmax, first)
                    for j in range(4):
                        cg = u * 4 + j
                        b = fm_unit_safe(slot, j, rhsHT, rbHT, N)
                        op("act", "activation", out=SG[:, cg, 0:N], in_=bk(b, N), func=AF.Tanh, scale=0.5,
                           reads=[bBK[b]], writes=[bSG[cg]])
                ckpt("v%d" % L)
                for cg in range(8):
                    g = cg // 2
                    b = bank()
                    for t in range(nt):
                        op("pe", "matmul", out=bk(b, 128, t * 128), lhsT=ONES[:, :], rhs=BR[:, var, L, g, :], start=(t == 0), stop=False, skip_group_check=True,
                           reads=[bC], writes=[bBK[b]], inc=(t == 0))
                    for t in range(nt):
                        op("pe", "matmul", out=bk(b, 128, t * 128), lhsT=VN[:, t, cg * 128:(cg + 1) * 128], rhs=WST[:, var, L, g, :], start=False, stop=(t == nt - 1), skip_group_check=True,
                           reads=[bVN[t], bC], writes=[bBK[b]] if t == nt - 1 else [], inc=(t == nt - 1))
                    op("dve", "scalar_tensor_tensor", out=UT[:, cg, 0:N], in0=bk(b, N), scalar=0.25, in1=UT[:, cg, 0:N], op0=ALU.mult, op1=ALU.mult,
                       reads=[bBK[b]], writes=[bUT[cg]])
                for u in range(2):
                    slot = wneed(ub + 8 + u, nmax, first)
                    for j in range(4):
                        cg = u * 4 + j
                        b = fm_unit_safe(slot, j, lambda kc: UT[:, kc, 0:N], lambda kc: [bUT[kc]], N)
                        op("dve", "scalar_tensor_tensor", out=MT[:, cg, 0:N], in0=SG[:, cg, 0:N], scalar=1.0, in1=bk(b, N), op0=ALU.add, op1=ALU.mult,
                           reads=[bSG[cg], bBK[b]], writes=[bMT[cg]])
                ckpt("pa%d" % L)
                blk["hist0"] = hist_load_inst(L, 0, blk["groups"][0])
                ckpt("q%d" % L)
                s0 = wneed(ub + 10, nmax, first)
                s1 = wneed(ub + 11, nmax, first, live=ub + 10)
                kpend = []
                for t in range(nt):
                    b = tm_pair((s0, s1), t, lhsHT, lbHT)
                    src = PS[:, b * 512:b * 512 + 1024]
                    k_ = st_cnt["ev"] % 2
                    st_cnt["ev"] += 1
                    i = st_cnt["i2"] % 2
                    st_cnt["i2"] += 1
                    ckpt("ka%d" % L)
                    op("act", "activation", out=STG[k_][:], in_=src, func=AF.Copy,
                       reads=[bBK[b], bBK[b + 1]], writes=[bSTG[k_]])
                    ckpt("kb%d" % L)
                    op("dve", "tensor_copy", out=SB16[i][:], in_=STG[k_][:],
                       reads=[bSTG[k_]], writes=[bSB16[i]])
                    ckpt("kc%d" % L)
                    op("sp", "dma_start", out=blk["nk"][L][t * 128:(t + 1) * 128, :], in_=STG[k_][:],
                       reads=[bSTG[k_]], dma="stg%d" % k_)
                    def ktr(t=t, i=i):
                        b2 = bank()
                        for hh in range(H):
                            op("pe", "transpose", out=PSB[:, b2 * 1024 + hh * 128:b2 * 1024 + (hh + 1) * 128], in_=SB16[i][:, hh * 128:(hh + 1) * 128], identity=IDB[:],
                               reads=[bSB16[i], bC], writes=[bBK[b2]], inc=True)
                        src_ps = PSB[:, b2 * 1024:(b2 + 1) * 1024].rearrange("p (h c) -> p h c", h=H)
                        op("dve", "tensor_copy", out=KT[:, :, t * 128:(t + 1) * 128], in_=src_ps,
                           reads=[bBK[b2]], writes=[bKT] + bVN + bMBT)
                    if kpend:
                        kpend.pop(0)()
                    kpend.append(ktr)
                while kpend:
                    kpend.pop(0)()
                ckpt("ky%d" % L)
                if blk["kv_store"]:
                    bKTS[(L, bi)] = Buf()
                    op("sp", "dma_start", out=KTS[L, :, :, bi * 512:(bi + 1) * 512].rearrange("h p c -> p h c"), in_=KT[:, :, :],
                       reads=[bKT], writes=[bKTS[(L, bi)]], dma="kts%d" % (bi % 2))
                ckpt("k%d" % L)
                s0 = wneed(ub + 12, nmax, first)
                s1 = wneed(ub + 13, nmax, first, live=ub + 12)
                for t in range(nt):
                    b = tm_pair((s0, s1), t, lhsHT, lbHT)
                    src = PS[:, b * 512:b * 512 + 1024]
                    k_ = st_cnt["ev"] % 2
                    st_cnt["ev"] += 1
                    op("act", "activation", out=STG[k_][:], in_=src, func=AF.Copy,
                       reads=[bBK[b], bBK[b + 1]], writes=[bSTG[k_]])
                    op("dve", "tensor_copy", out=VA[:, :, t, 0:128], in_=STG[k_][:].rearrange("p (h c) -> p h c", h=H),
                       reads=[bSTG[k_]], writes=[bVA[t]])
                    op("sp", "dma_start", out=blk["nv"][L][t * 128:(t + 1) * 128, :], in_=STG[k_][:],
                       reads=[bSTG[k_]], dma="stg%d" % k_)
                if blk["kv_store"]:
                    bVSS[(L, bi)] = Buf()
                    op("sp", "dma_start", out=VSS[L, :, :, bi * 4:(bi + 1) * 4, :].rearrange("h p t c -> p h t c"), in_=VA[:],
                       reads=bVA, writes=[bVSS[(L, bi)]], dma="vss%d" % (bi % 2))
                for u in range(2):
                    slot = wneed(ub + 14 + u, nmax, first)
                    for j in range(4):
                        hh = u * 4 + j
                        b = fm_unit_safe(slot, j, rhsHT, rbHT, N)
                        if hh % 2 == 0:
                            op("act", "activation", out=QT[:, hh, 0:N], in_=bk(b, N), func=AF.Copy,
                               reads=[bBK[b]], writes=[bQT[hh]] + bUT)
                        else:
                            op("dve", "tensor_copy", out=QT[:, hh, 0:N], in_=bk(b, N),
                               reads=[bBK[b]], writes=[bQT[hh]] + bUT)
                s0 = wneed(ub + 16, nmax, first)
                s1 = wneed(ub + 17, nmax, first, live=ub + 16)
                for t in range(nt):
                    b = tm_pair((s0, s1), t, lhsHT, lbHT)
                    src = PS[:, b * 512:b * 512 + 1024]
                    i = st_cnt["i2"] % 2
                    st_cnt["i2"] += 1
                    op("act", "activation", out=TBF[i][:], in_=src, func=AF.Tanh, scale=0.5,
                       reads=[bBK[b], bBK[b + 1]], writes=[bTB[i]])
                    op("dve", "scalar_tensor_tensor", out=TA[i][:], in0=TBF[i][:], scalar=1.0, in1=src, op0=ALU.add, op1=ALU.mult,
                       reads=[bTB[i], bBK[b], bBK[b + 1]], writes=[bTA[i]])
                    op("pool", "tensor_tensor", out=GZ[:, t, :].rearrange("p (h c) -> p h c", h=H), in0=TA[i][:].rearrange("p (h c) -> p h c", h=H), in1=AG[L][:].unsqueeze(1).to_broadcast([128, H, 128]), op=ALU.mult,
                       reads=[bTA[i], bC], writes=[bGZ[t]])
                for u in range(2):
                    slot = wneed(ub + 18 + u, nmax, first)
                    for j in range(4):
                        cg = u * 4 + j
                        b = fm_unit_safe(slot, j, rhsHT, rbHT, N)
                        op("act", "activation", out=SG[:, cg, 0:N], in_=bk(b, N), func=AF.Tanh, scale=0.5,
                           reads=[bBK[b]], writes=[bSG[cg]])
                ckpt("gb%d" % L)
                attention(L, blk)
                ckpt("at%d" % L)
                tb = [bank() for _ in range(4)]
                for t in range(nt):
                    for hh in range(H):
                        b = tb[hh // 2]
                        c0 = b * 1024 + (hh % 2) * 512 + t * 128
                        op("pe", "transpose", out=PSB[:, c0:c0 + 128], in_=YB[:, t, hh * 128:(hh + 1) * 128], identity=IDB[:],
                           reads=[bYB[t], bC], writes=[bBK[b]], inc=True)
                for hh in range(H):
                    b = tb[hh // 2]
                    c0 = b * 1024 + (hh % 2) * 512
                    if hh % 2 == 0:
                        op("act", "activation", out=YBT[:, hh, 0:N], in_=PSB[:, c0:c0 + N], func=AF.Copy,
                           reads=[bBK[b]], writes=[bUT[hh]] + bQT)
                    else:
                        op("dve", "tensor_copy", out=YBT[:, hh, 0:N], in_=PSB[:, c0:c0 + N],
                           reads=[bBK[b]], writes=[bUT[hh]] + bQT)
                for u in range(2):
                    slot = wneed(ub + 20 + u, nmax, first)
                    for j in range(4):
                        cg = u * 4 + j
                        b = fm_unit_safe(slot, j, lambda kc: YBT[:, kc, 0:N], lambda kc: [bUT[kc]], N)
                        i = st_cnt["i2"] % 2
                        st_cnt["i2"] += 1
                        op("dve", "scalar_tensor_tensor", out=TA[i][:, 0:N], in0=SG[:, cg, 0:N], scalar=1.0, in1=bk(b, N), op0=ALU.add, op1=ALU.mult,
                           reads=[bSG[cg], bBK[b]], writes=[bTA[i]])
                        op("pool", "tensor_tensor", out=MBT[:, cg, 0:N], in0=TA[i][:, 0:N], in1=MT[:, cg, 0:N], op=ALU.add,
                           reads=[bTA[i], bMT[cg]], writes=[bMBT[cg], bKT] + bVN)
                ckpt("pb%d" % L)
                s0 = wneed(ub + 22, nmax, first)
                s1 = wneed(ub + 23, nmax, first, live=ub + 22)
                for t in range(nt):
                    b = tm_pair((s0, s1), t, lambda kc, t: MBT[:, kc, t * 128:(t + 1) * 128], lambda kc, t: [bMBT[kc]])
                    src = PS[:, b * 512:b * 512 + 1024]
                    for u2 in range(2):
                        op("dve", "scalar_tensor_tensor", out=X[:, t, u2 * 512:(u2 + 1) * 512], in0=bk(b + u2), scalar=0.5,
                           in1=X[:, t, u2 * 512:(u2 + 1) * 512], op0=ALU.mult, op1=ALU.add,
                           reads=[bBK[b + u2]], writes=[bX[t]] if u2 == 0 else [], adds=[] if u2 == 0 else [bX[t]])
            op("sp", "dma_start", out=GVB[:], in_=final_g.partition_broadcast(128), writes=[bGV], dma="gv")
            for t in range(nt):
                i = st_cnt["i2"] % 2
                st_cnt["i2"] += 1
                op("act", "activation", out=SB16[i][:], in_=X[:, t, :], func=AF.Square, accum_out=SSQ[:, t:t + 1],
                   reads=[bX[t]], writes=[bSB16[i]] + ([bSSQ] if t == 0 else []), adds=[] if t == 0 else [bSSQ])
            rsqrt(SSQ[:, 0:nt], RSQ[:, 0:nt], None, 1.0 / D, EPS, [bSSQ], [bSSQ])
            for t in range(nt):
                k_ = st_cnt["ev"] % 2
                st_cnt["ev"] += 1
                op("dve", "scalar_tensor_tensor", out=STG[k_][:], in0=X[:, t, :], scalar=RSQ[:, t:t + 1], in1=FG[:], op0=ALU.mult, op1=ALU.mult,
                   reads=[bX[t], bSSQ, bC, bGV], writes=[bSTG[k_]])
                op("sp", "dma_start", out=blk["y"][t * 128:(t + 1) * 128, :], in_=STG[k_][:],
                   reads=[bSTG[k_]], dma="stg%d" % k_)

        def prompt_hist(bi):
            def f(L, h):
                nh = bi * 4
                kb = [bKTS[(L, b_)] for b_ in range(bi)]
                vb = [bVSS[(L, b_)] for b_ in range(bi)]
                return KTS[L, h, :, 0:nh * 128], kb, VSS[L, h, :, 0:nh, :], vb
            return f

        def sample_hist(s):
            def f(L, h):
                kb = [bKTC[k_] for k_ in bKTC if k_[0] == L and k_[1] == s]
                vb = [bVC[(L, s, h)]]
                return KTC[L, s, h, :, :], kb, VC[L, s, h, :, :, :], vb
            return f

        blocks = []
        for bi in range(NB):
            blocks.append(dict(nt=4, var=0, x=xp[bi * 512:(bi + 1) * 512, :].rearrange("(t p) d -> p t d", p=128),
                               y=yp[bi * 512:(bi + 1) * 512, :],
                               nk=[nkp[L, bi * 512:(bi + 1) * 512, :] for L in range(2)],
                               nv=[nvp[L, bi * 512:(bi + 1) * 512, :] for L in range(2)],
                               kv_store=(bi < NB - 1),
                               groups=[dict(qc0=0, nq=512, qp0=0, nqr=128, nqt=4, nh=bi * 4, hist=prompt_hist(bi), causal=True)]))
        blocks.append(dict(nt=1, var=1, x=xs.rearrange("(t p) d -> p t d", p=128), y=ys,
                           nk=[nks[L] for L in range(2)], nv=[nvs[L] for L in range(2)], kv_store=False,
                           groups=[dict(qc0=s * 64, nq=64, qp0=s * 64, nqr=64, nqt=1, nh=NTP, hist=sample_hist(s), causal=False, oslot=s,
                                        cur=(s * 64, 64)) for s in range(2)]))
        nbt = len(blocks)
        v_items = [it for it in pre_items if it[0] == "v"]
        k_items = [it for it in pre_items if it[0] == "k"]
        pv_per = (len(v_items) + NB - 1) // NB
        pk_per = (len(k_items) + NB - 1) // NB
        try:
            for bi, blk in enumerate(blocks):
                kk = k_items[bi * pk_per:(bi + 1) * pk_per] if bi < NB else []
                blk["pre"] = (v_items[bi * pv_per:(bi + 1) * pv_per] if bi < NB else []) + kk[0:(len(kk) + 1) // 2]
                blk["pre_b"] = kk[(len(kk) + 1) // 2:]
                do_block(blk, bi, nbt)
        except _Stop:
            pass
        S_.emit()
    return nc


_CACHE = {}


def _consts():
    ident = np.eye(128, dtype=np.float32)
    t = np.arange(128)
    tril0 = (t[None, :] <= t[:, None]).astype(np.float32)
    same = (t[None, :] // 64) == (t[:, None] // 64)
    tril1 = (tril0 * same).astype(np.float32)
    return ident, np.stack([tril0, tril1])


def kernel(x_prompt, x_sample, cache_k, cache_v, norm_g, w_in, w_s, b_s, v_norm_g, lam_q1, lam_k1, lam_q2, lam_k2,
           attn_norm_g, w_pa, w_pb, w_out, final_norm_g):
    f = lambda a: np.ascontiguousarray(np.asarray(a, dtype=np.float32))
    x_prompt, x_sample, cache_k, cache_v = f(x_prompt), f(x_sample), f(cache_k), f(cache_v)
    B, S, _ = x_prompt.shape
    DB, DS, _ = x_sample.shape
    P = cache_k.shape[2]
    ncores = B
    assert DB == 2 * ncores and DS == 64
    key = (S, P)
    if key not in _CACHE:
        _CACHE[key] = build(S, P)
    nc = _CACHE[key]
    ident, tril = _consts()
    shared = {"norm_g": f(norm_g), "w_in": f(w_in), "w_pa": f(w_pa), "w_pb": f(w_pb), "w_out": f(w_out), "w_s": f(w_s),
              "b_s": f(b_s), "v_norm_g": f(v_norm_g), "lam_q1": f(lam_q1), "lam_k1": f(lam_k1), "lam_q2": f(lam_q2),
              "lam_k2": f(lam_k2), "attn_norm_g": f(attn_norm_g), "final_norm_g": f(final_norm_g), "ident": ident, "tril": tril}
    in_maps = []
    for c in range(ncores):
        m = dict(shared)
        m["xp"] = x_prompt[c]
        m["xs"] = x_sample[2 * c:2 * c + 2].reshape(128, D)
        m["ck"] = np.ascontiguousarray(cache_k[:, 2 * c:2 * c + 2].reshape(2, 2, P, D))
        m["cv"] = np.ascontiguousarray(cache_v[:, 2 * c:2 * c + 2].reshape(2, 2, P, D))
        in_maps.append(m)
    res = run_bass_kernel_spmd(nc, in_maps, core_ids=list(range(ncores)))
    r = res.results
    y_prompt = np.stack([r[c]["yp"] for c in range(ncores)], 0)
    y_sample = np.concatenate([r[c]["ys"].reshape(2, 64, D) for c in range(ncores)], 0)
    nkp = np.stack([r[c]["nkp"] for c in range(ncores)], 1).reshape(2, B, S, H, 128)
    nvp = np.stack([r[c]["nvp"] for c in range(ncores)], 1).reshape(2, B, S, H, 128)
    nks = np.concatenate([r[c]["nks"].reshape(2, 2, 64, H, 128) for c in range(ncores)], 1)
    nvs = np.concatenate([r[c]["nvs"].reshape(2, 2, 64, H, 128) for c in range(ncores)], 1)
    nms = np.concatenate([r[c]["nms"].reshape(2, 2, 64, D) for c in range(ncores)], 1)
    return (y_prompt.astype(np.float32), y_sample.astype(np.float32), nkp.astype(np.float32), nvp.astype(np.float32),
            nks.astype(np.float32), nvs.astype(np.float32), nms.astype(np.float32))
```

```python
import contextlib
import os
import math
import numpy as np
import concourse.bass as bass
import concourse.mybir as mybir
from concourse.bass_utils import run_bass_kernel_spmd

F32 = mybir.dt.float32
BF16 = mybir.dt.bfloat16
I32 = mybir.dt.int32
AF = mybir.ActivationFunctionType
ALU = mybir.AluOpType

D = 1024
H = 8
NCOL = 9216
EPS = 1e-6
KGELU = 0.7978845608028654
ENGS = ("pe", "act", "dve", "pool", "sp")


class _Stop(Exception):
    pass


def ckpt(name):
    if os.environ.get("KSTOP") == name:
        raise _Stop()


class Buf:
    __slots__ = ("w", "r", "name")

    def __init__(self, name=""):
        self.w = []
        self.r = []
        self.name = name


class Sched:
    def __init__(self, nc):
        self.nc = nc
        self.ops = {e: [] for e in ENGS}
        self.cnt = {}
        self.seen = {e: {} for e in ENGS}
        self.semnames = []
        for e in ENGS:
            self._sem("E_" + e)

    def _sem(self, name):
        if name not in self.cnt:
            self.cnt[name] = 0
            self.semnames.append(name)
        return name

    def op(self, eng, name, reads=(), writes=(), inc=True, dma=None, adds=(), **kw):
        fn = (name, kw)
        toks = []
        for b in reads:
            toks.extend(b.w)
        for b in writes:
            toks.extend(b.w)
            toks.extend(b.r)
        need = {}
        for t in toks:
            s, v = t
            if eng == "pe" and s == "E_pe":
                continue
            if self.seen[eng].get(s, 0) >= v:
                continue
            if need.get(s, 0) < v:
                need[s] = v
        for s, v in need.items():
            self.seen[eng][s] = v
        tok = None
        incspec = None
        if dma is not None:
            s = self._sem("D_" + dma)
            self.cnt[s] += 16
            tok = (s, self.cnt[s])
            incspec = (s, 16)
        elif inc:
            s = "E_" + eng
            self.cnt[s] += 1
            tok = (s, self.cnt[s])
            incspec = (s, 1)
        self.ops[eng].append((fn, list(need.items()), incspec))
        if tok is not None:
            for b in reads:
                b.r.append(tok)
                if len(b.r) > 64:
                    b.r = _compact(b.r)
            for b in writes:
                b.w = [tok]
                b.r = []
            for b in adds:
                b.w.append(tok)
                if len(b.w) > 64:
                    b.w = _compact(b.w)
        return tok

    def emit(self):
        nc = self.nc
        with contextlib.ExitStack() as st:
            sems = {}
            for n in self.semnames:
                sems[n] = st.enter_context(nc.semaphore(n))
            block = st.enter_context(nc.Block())
            engmap = {"pe": block.tensor, "act": block.scalar, "dve": block.vector,
                      "pool": block.gpsimd, "sp": block.sync}
            finals = [(n, self.cnt[n]) for n in self.semnames if self.cnt[n] > 0]

            def make(engname):
                oplist = self.ops[engname]

                def body(eng):
                    for fn, waits, incspec in oplist:
                        for s, v in waits:
                            eng.wait_ge(sems[s], v)
                        ins = getattr(eng, fn[0])(**fn[1])
                        if incspec is not None:
                            ins.then_inc(sems[incspec[0]], incspec[1])
                    if engname == "sp":
                        for s, v in finals:
                            eng.wait_ge(sems[s], v)
                return body

            for e in ENGS:
                engmap[e](make(e))


def _compact(toks):
    m = {}
    for s, v in toks:
        if m.get(s, 0) < v:
            m[s] = v
    return list(m.items())


def _unit_src(u):
    j = u % 2
    k = u // 2
    tab = [("w_in", 1024), ("w_in", 0), ("w_in", 2048), ("w_in", 7168), ("w_pa", 0),
           ("w_in", 4096), ("w_in", 5120), ("w_in", 3072), ("w_in", 6144), ("w_in", 8192),
           ("w_pb", 0), ("w_out", 0)]
    n, c = tab[k]
    return n, c + 512 * j


def build(S, P):
    NB = S // 512
    NTP = P // 128
    nc = bass.Bass("TRN2", target_bir_lowering=False)

    def din(name, shape, dt=F32):
        return nc.dram_tensor(name, shape, dt, kind="ExternalInput").ap()

    def dout(name, shape):
        return nc.dram_tensor(name, shape, F32, kind="ExternalOutput").ap()

    def dscr(name, shape, dt=BF16):
        return nc.dram_tensor(name, shape, dt, kind="Internal").ap()

    xp = din("xp", [S, D])
    xs = din("xs", [128, D])
    ck = din("ck", [2, 2, P, D])
    cv = din("cv", [2, 2, P, D])
    norm_g = din("norm_g", [2, D])
    wts = {"w_in": din("w_in", [2, D, NCOL]), "w_pa": din("w_pa", [2, D, D]),
           "w_pb": din("w_pb", [2, D, D]), "w_out": din("w_out", [2, D, D])}
    w_s = din("w_s", [2, 4, 128, 128])
    b_s = din("b_s", [2, 4, 128])
    v_norm_g = din("v_norm_g", [2, D])
    lamv = [din(n, [2, 64]) for n in ("lam_q1", "lam_k1", "lam_q2", "lam_k2")]
    attn_g = din("attn_norm_g", [2, 128])
    final_g = din("final_norm_g", [D])
    ident = din("ident", [128, 128])
    tril = din("tril", [2, 128, 128])

    yp = dout("yp", [S, D])
    ys = dout("ys", [128, D])
    nkp = dout("nkp", [2, S, D])
    nvp = dout("nvp", [2, S, D])
    nks = dout("nks", [2, 128, D])
    nvs = dout("nvs", [2, 128, D])
    nms = dout("nms", [2, 128, D])

    WS = dscr("WS", [2, 24, 128, 8, 512])
    KTS = dscr("KTS", [2, H, 128, S])
    VSS = dscr("VSS", [2, H, 128, S // 128, 129])
    KTC = dscr("KTC", [2, 2, H, 128, P])
    VC = dscr("VC", [2, 2, H, 128, NTP, 129])

    S_ = Sched(nc)
    op = S_.op
    NHMAX = max((NB - 1) * 4, NTP, 1)

    with contextlib.ExitStack() as st:
        def sb(name, shape, dt):
            return st.enter_context(nc.sbuf_tensor(name, shape, dt))

        X = sb("X", [128, 4, D], F32)
        HT = sb("HT", [128, 8, 512], BF16)
        YB = HT[:].rearrange("p a b -> p (a b)").rearrange("p (t c) -> p t c", t=4)
        UT = sb("UT", [128, 8, 512], BF16)
        YBT = UT
        VN = sb("VN", [128, 4, D], BF16)
        KT = VN[:].rearrange("p a b -> p (a b)").rearrange("p (h c) -> p h c", h=8)
        MBT = KT
        SG = sb("SG", [128, 8, 512], BF16)
        MT = sb("MT", [128, 8, 512], BF16)
        QT = UT
        VA = sb("VA", [128, 8, 4, 129], BF16)
        GZ = sb("GZ", [128, 4, D], BF16)
        KH = [sb("KH%d" % i, [128, NHMAX * 128], BF16) for i in range(2)]
        VH = [sb("VH%d" % i, [128, NHMAX, 129], BF16) for i in range(2)]
        NPT = 4
        PT = [sb("PT%d" % i, [128, 2, 512], BF16) for i in range(NPT)]
        PD = [sb("PD%d" % i, [128, 2, 512], BF16) for i in range(4)]
        NW = 4
        W = [sb("W%d" % i, [128, 8, 512], BF16) for i in range(NW)]
        STG = [sb("STG%d" % i, [128, D], F32) for i in range(2)]
        TA = [sb("TA%d" % i, [128, D], F32) for i in range(2)]
        TBF = [sb("TBF%d" % i, [128, D], F32) for i in range(2)]
        SB16 = [sb("SB16_%d" % i, [128, D], BF16) for i in range(2)]
        OS = [sb("OS%d" % i, [128, 2, 4, 129], F32) for i in range(2)]
        OA1 = sb("OA", [128, 4, 128], F32)
        GVB = sb("GVB", [128, D], F32)
        GV = [GVB, GVB]
        FG = GVB
        AG = [sb("AG%d" % i, [128, 128], F32) for i in range(2)]
        NG = sb("NG", [128, 2, 8], F32)
        LQ = TBF[0][:, 0:512].rearrange("p (a b) -> p a b", a=4)
        LS = sb("LS", [128, 8], F32)
        NLAM = sb("NLAM", [128, 2], F32)
        IDF = TA[1][:, 0:128]
        IDB = sb("IDB", [128, 128], BF16)
        TRL = sb("TRL", [128, 2, 128], F32)
        WSF = TA[1][:, 128:256]
        WST = sb("WST", [128, 2, 2, 4, 128], BF16)
        BFT = TA[0][0:33, :].rearrange("p (l g t) -> p l g t", l=2, g=4)
        BHI = TBF[1][0:33, 0:512].bitcast(BF16).rearrange("p (l g t) -> p l g t", l=2, g=4)
        BLO = TBF[1][0:33, 512:1024].bitcast(BF16).rearrange("p (l g t) -> p l g t", l=2, g=4)
        BR = sb("BR", [33, 2, 2, 4, 128], BF16)
        ONES = sb("ONES", [33, 128], BF16)
        SSQ = sb("SSQ", [128, 4], F32)
        RSQ = sb("RSQ", [128, 4], F32)
        SSV = sb("SSV", [128, 4], F32)
        RSV = sb("RSV", [128, 4], F32)
        SSO = sb("SSO", [128, 8, 4], F32)
        RSO = sb("RSO", [128, 8, 4], F32)
        RL = sb("RL", [128, 2, 4], F32)
        RQF = sb("RQF", [128, 3, 32], F32)
        RQI = sb("RQI", [128, 32], I32)
        RQY = sb("RQY", [128, 32], F32)
        PS = st.enter_context(nc.psum_tensor("PS", [128, 8 * 512], F32))
        PSB = PS[:].bitcast(BF16)

        bX = [Buf() for _ in range(4)]
        bHT = [Buf() for _ in range(8)]
        bYB = [Buf() for _ in range(4)]
        bUT = [Buf() for _ in range(8)]
        bVN = [Buf() for _ in range(4)]
        bKT = Buf()
        bMBT = [Buf() for _ in range(8)]
        bSG = [Buf() for _ in range(8)]
        bMT = [Buf() for _ in range(8)]
        bQT = [Buf() for _ in range(8)]
        bVA = [Buf() for _ in range(4)]
        bGZ = [Buf() for _ in range(4)]
        bKH = [Buf() for _ in range(2)]
        bVH = [Buf() for _ in range(2)]
        bPT = [Buf() for _ in range(NPT)]
        bPT2 = [[Buf(), Buf()] for _ in range(NPT)]
        bPD = [Buf() for _ in range(4)]
        bW = [Buf() for _ in range(NW)]
        bSTG = [Buf() for _ in range(2)]
        bTA = [Buf() for _ in range(2)]
        bTB = [Buf() for _ in range(2)]
        bSB16 = [Buf() for _ in range(2)]
        bOS = [Buf() for _ in range(2)]
        bOA = Buf()
        bJ = Buf()
        bGV = Buf()
        bBK = [Buf() for _ in range(8)]
        bC = Buf()
        bSSQ = Buf(); bSSV = Buf(); bSSO = Buf(); bRL = Buf(); bRQ = Buf()
        bSSOh = [Buf() for _ in range(H)]
        bWS = {}
        bKTS = {}
        bVSS = {}
        bKTC = {}
        bVC = {}

        st_cnt = {"bank": 0, "pair": 0, "i2": 0, "pt": 0, "hs": 0, "wn": 0, "ev": 0}

        def bank():
            b = st_cnt["bank"] % 8
            st_cnt["bank"] += 1
            return b

        def pair():
            b = (st_cnt["pair"] % 4) * 2
            st_cnt["pair"] += 1
            return b

        def bk(b, n=512, c0=0):
            return PS[:, b * 512 + c0:b * 512 + c0 + n]

        op("sp", "dma_start", out=IDF[:], in_=ident, writes=[bC], dma="c0")
        op("sp", "dma_start", out=TRL[:], in_=tril.rearrange("v t s -> t v s"), writes=[bC], dma="c0")
        for L in range(2):
            op("sp", "dma_start", out=NG[:, L, :], in_=norm_g[L].rearrange("(kc p) -> p kc", p=128), allow_slow_non_contiguous=True, writes=[bC], dma="c0")
            op("sp", "dma_start", out=AG[L][:], in_=attn_g[L].partition_broadcast(128), writes=[bC], dma="c0")
        for i in range(4):
            op("sp", "dma_start", out=LQ[:, i, :], in_=lamv[i].rearrange("l d -> (l d)").partition_broadcast(128),
               writes=[bC], dma="c0")
        op("dve", "memset", ap=BFT[:], constant=0.0, writes=[bC])
        op("sp", "dma_start", out=BFT[0:1], in_=b_s.rearrange("(o l) g t -> o l g t", o=1), writes=[bC], dma="c0")
        op("sp", "dma_start", out=BFT[32:33], in_=b_s.rearrange("(o l) g t -> o l g t", o=1), writes=[bC], dma="c0")
        op("dve", "tensor_copy", out=IDB[:], in_=IDF[:], reads=[bC], writes=[bC])
        for L in range(2):
            lam_init = 0.8 - 0.6 * math.exp(-0.3 * L)
            op("dve", "tensor_scalar", out=AG[L][:], in0=AG[L][:], scalar1=0.5 * (1.0 - lam_init), scalar2=None, op0=ALU.mult, reads=[bC], writes=[bC])
        for w_ in range(2):
            op("dve", "tensor_tensor", out=LQ[:, 2 * w_, :], in0=LQ[:, 2 * w_, :], in1=LQ[:, 2 * w_ + 1, :], op=ALU.mult, reads=[bC], writes=[bC])
            op("dve", "reduce_sum", out=LS[:, 2 * w_:2 * w_ + 2], in_=LQ[:, 2 * w_, :].rearrange("p (l d) -> p l d", l=2), axis=mybir.AxisListType.X, reads=[bC], writes=[bC])
        op("act", "activation", out=LS[:, 4:8], in_=LS[:, 0:4], func=AF.Exp, reads=[bC], writes=[bC])
        op("dve", "tensor_tensor", out=NLAM[:], in0=LS[:, 6:8], in1=LS[:, 4:6], op=ALU.subtract, reads=[bC], writes=[bC])
        for L in range(2):
            lam_init = 0.8 - 0.6 * math.exp(-0.3 * L)
            op("dve", "tensor_scalar", out=NLAM[:, L:L + 1], in0=NLAM[:, L:L + 1], scalar1=-lam_init, scalar2=None, op0=ALU.add, reads=[bC], writes=[bC])
        op("dve", "memset", ap=ONES[:], constant=0.0, writes=[bC])
        op("dve", "memset", ap=ONES[0:1], constant=1.0, writes=[bC])
        op("dve", "memset", ap=ONES[32:33], constant=1.0, writes=[bC])
        op("dve", "memset", ap=BR[:], constant=0.0, writes=[bC])
        op("dve", "tensor_copy", out=BHI[:], in_=BFT[:], reads=[bC], writes=[bC])
        op("dve", "tensor_tensor", out=BLO[:], in0=BFT[:], in1=BHI[:], op=ALU.subtract, reads=[bC], writes=[bC])
        op("dve", "tensor_copy", out=BR[0:1, 0], in_=BHI[0:1], reads=[bC], writes=[bC])
        op("dve", "tensor_copy", out=BR[32:33, 0], in_=BLO[32:33], reads=[bC], writes=[bC])
        for hf in range(2):
            op("dve", "tensor_copy", out=BR[0:1, 1, :, :, hf * 64:(hf + 1) * 64], in_=BHI[0:1, :, :, 0:64],
               reads=[bC], writes=[bC])
            op("dve", "tensor_copy", out=BR[32:33, 1, :, :, hf * 64:(hf + 1) * 64], in_=BLO[32:33, :, :, 0:64],
               reads=[bC], writes=[bC])
        op("dve", "memset", ap=VA[:, :, :, 128:129], constant=1.0, writes=bVA)
        for i in range(2):
            op("dve", "memset", ap=VH[i][:, :, 128:129], constant=1.0, writes=[bVH[i]])
        for j in range(4):
            op("dve", "memset", ap=PD[j][:], constant=0.0, writes=[bPD[j]])
        for j in range(NPT):
            op("dve", "memset", ap=PT[j][:], constant=0.0, writes=[bPT[j]])
        for var in range(2):
            for L in range(2):
                for g in range(4):
                    if var == 0:
                        op("sp", "dma_start", out=WSF[:], in_=w_s[L, g], writes=[bTA[0]], dma="c1")
                    else:
                        op("dve", "memset", ap=WSF[:], constant=0.0, writes=[bTA[0]])
                        op("sp", "dma_start", out=WSF[0:64, 0:64], in_=w_s[L, g, 0:64, 0:64],
                           writes=[bTA[0]], dma="c1")
                        op("sp", "dma_start", out=WSF[64:128, 64:128], in_=w_s[L, g, 0:64, 0:64],
                           reads=[bTA[0]], dma="c1")
                        bTA[0].w = bTA[0].w + bTA[0].r
                        bTA[0].r = []
                    op("dve", "tensor_tensor", out=SB16[0][:, 0:128], in0=WSF[:], in1=TRL[:, var, :], op=ALU.mult,
                       reads=[bTA[0], bC], writes=[bSB16[0]])
                    b = bank()
                    op("pe", "transpose", out=PSB[:, b * 1024:b * 1024 + 128], in_=SB16[0][:, 0:128], identity=IDB[:],
                       reads=[bSB16[0], bC], writes=[bBK[b]])
                    op("dve", "tensor_copy", out=WST[:, var, L, g, :], in_=PSB[:, b * 1024:b * 1024 + 128],
                       reads=[bBK[b]], writes=[bC])

        wstate = {"issued": 0}

        def wissue(n, first_block):
            L = (n % 48) // 24
            u = n % 24
            slot = n % NW
            if n < 48:
                name, c0 = _unit_src(u)
                src = wts[name][L][:, c0:c0 + 512].rearrange("(kc p) n -> p kc n", p=128)
                op("pool", "dma_start", out=W[slot][:], in_=src, writes=[bW[slot]], dma="wp%d" % slot)
                bWS[(L, u)] = Buf()
                op("sp", "dma_start", out=WS[L, u], in_=W[slot][:], reads=[bW[slot]], writes=[bWS[(L, u)]],
                   dma="ws%d" % slot)
            else:
                op("sp", "dma_start", out=W[slot][:], in_=WS[L, u], reads=[bWS[(L, u)]], writes=[bW[slot]],
                   dma="w%d" % slot)

        def wneed(n, nmax, first_block, live=None):
            live = n if live is None else live
            while wstate["issued"] <= min(live + NW - 1, nmax):
                wissue(wstate["issued"], first_block)
                wstate["issued"] += 1
            return n % NW

        def prepass_v(L, s, h):
            sl = st_cnt["hs"] % 2
            st_cnt["hs"] += 1
            src = cv[L, s].rearrange("(t p) (h e) -> p h t e", p=128, h=H)[:, h]
            op("pool", "dma_start", out=VH[sl][:, 0:NTP, 0:128], in_=src, writes=[bVH[sl]], dma="vhp%d" % sl)
            bVC[(L, s, h)] = Buf()
            op("sp", "dma_start", out=VC[L, s, h], in_=VH[sl][:, 0:NTP, :], reads=[bVH[sl]], writes=[bVC[(L, s, h)]],
               dma="vc%d" % sl)

        def prepass_k(L, s, t0, ntl):
            pbufs = [(TA[0][:].bitcast(BF16)[:, 0:D], bTA[0]), (TA[1][:].bitcast(BF16)[:, 0:D], bTA[1]),
                     (TBF[0][:].bitcast(BF16)[:, 0:D], bTB[0]), (TBF[1][:].bitcast(BF16)[:, 0:D], bTB[1])]
            for tt in range(ntl):
                pb_, pbb = pbufs[tt]
                src = ck[L, s, (t0 + tt) * 128:(t0 + tt + 1) * 128, :]
                op("pool", "dma_start", out=pb_, in_=src, writes=[pbb], dma="pk%d" % tt)
            for tt in range(ntl):
                pb_, pbb = pbufs[tt]
                b = bank()
                for h in range(H):
                    op("pe", "transpose", out=PSB[:, b * 1024 + h * 128:b * 1024 + (h + 1) * 128], in_=pb_[:, h * 128:(h + 1) * 128], identity=IDB[:],
                       reads=[pbb, bC], writes=[bBK[b]], inc=True)
                src_ps = PSB[:, b * 1024:(b + 1) * 1024].rearrange("p (h c) -> p h c", h=H)
                if tt % 2 == 0:
                    op("act", "activation", out=KT[:, :, tt * 128:(tt + 1) * 128], in_=src_ps, func=AF.Copy,
                       reads=[bBK[b]], writes=[bKT] + bVN + bMBT)
                else:
                    op("dve", "tensor_copy", out=KT[:, :, tt * 128:(tt + 1) * 128], in_=src_ps,
                       reads=[bBK[b]], writes=[bKT] + bVN + bMBT)
            key = (L, s, t0)
            bKTC[key] = Buf()
            op("sp", "dma_start", out=KTC[L, s, :, :, t0 * 128:(t0 + ntl) * 128].rearrange("h p c -> p h c"), in_=KT[:, :, 0:ntl * 128], reads=[bKT], writes=[bKTC[key]], dma="ktc")

        pre_items = []
        for L in range(2):
            for s in range(2):
                for h in range(H):
                    pre_items.append(("v", L, s, h))
                for t0 in range(0, NTP, 4):
                    pre_items.append(("k", L, s, t0, min(4, NTP - t0)))

        def run_pre(items):
            for it in items:
                if it[0] == "v":
                    prepass_v(*it[1:])
                else:
                    prepass_k(*it[1:])

        def rsqrt(src, dst, n, scale, eps, rbufs, wbufs):
            shp = list(src.shape)
            tot = 1
            for d_ in shp[1:]:
                tot *= d_

            def v(ap):
                if len(shp) == 3:
                    return ap[:, 0:tot].rearrange("p (a b) -> p a b", a=shp[1])
                return ap[:, 0:tot]
            t = v(RQF[:, 0, :]); a = v(RQF[:, 1, :]); c = v(RQF[:, 2, :])
            yi = v(RQY[:].bitcast(I32))
            y = v(RQY[:, :])
            ti = v(RQF[:, 0, :].bitcast(I32))
            op("dve", "tensor_scalar", out=t, in0=src, scalar1=scale, scalar2=eps, op0=ALU.mult, op1=ALU.add,
               reads=rbufs, writes=[bRQ])
            op("dve", "tensor_copy", out=a, in_=ti, writes=[bRQ])
            op("dve", "tensor_scalar", out=a, in0=a, scalar1=-0.5, scalar2=float(0x5f3759df), op0=ALU.mult, op1=ALU.add, writes=[bRQ])
            op("dve", "tensor_copy", out=yi, in_=a, writes=[bRQ])
            for it in range(3):
                op("dve", "tensor_tensor", out=a, in0=y, in1=y, op=ALU.mult, writes=[bRQ])
                op("dve", "tensor_tensor", out=c, in0=a, in1=t, op=ALU.mult, writes=[bRQ])
                op("dve", "tensor_scalar", out=c, in0=c, scalar1=-0.5, scalar2=1.5, op0=ALU.mult, op1=ALU.add,
                   writes=[bRQ])
                if it < 2:
                    op("dve", "tensor_tensor", out=y, in0=y, in1=c, op=ALU.mult, writes=[bRQ])
                else:
                    op("dve", "tensor_tensor", out=dst, in0=y, in1=c, op=ALU.mult, reads=[bRQ], writes=wbufs)

        def hist_load_inst(L, h, g):
            nh = g["nh"]
            if nh == 0:
                return None
            sl = st_cnt["hs"] % 2
            st_cnt["hs"] += 1
            ksrc, kb, vsrc, vb = g["hist"](L, h)
            op("sp", "dma_start", out=KH[sl][:, 0:nh * 128], in_=ksrc, reads=kb, writes=[bKH[sl]], dma="kh%d" % sl)
            op("sp", "dma_start", out=VH[sl][:, 0:nh, :], in_=vsrc, reads=vb, writes=[bVH[sl]], dma="vh%d" % sl)
            return sl

        def attention(L, blk):
            nt = blk["nt"]
            N = nt * 128
            var = blk["var"]
            insts = []
            for h in range(H):
                for g in blk["groups"]:
                    insts.append((h, g))

            def hist_load(idx):
                return hist_load_inst(L, *insts[idx])

            def _unused(idx):
                h, g = insts[idx]
                nh = g["nh"]
                if nh == 0:
                    return None
                sl = st_cnt["hs"] % 2
                st_cnt["hs"] += 1
                ksrc, kb, vsrc, vb = g["hist"](L, h)
                op("sp", "dma_start", out=KH[sl][:, 0:nh * 128], in_=ksrc, reads=kb, writes=[bKH[sl]], dma="kh%d" % sl)
                op("sp", "dma_start", out=VH[sl][:, 0:nh, :], in_=vsrc, reads=vb, writes=[bVH[sl]], dma="vh%d" % sl)
                return sl

            OACC = PS[:, 2048:4096].rearrange("p (m t c) -> p m t c", m=2, t=4, c=256)
            slots = {0: blk.pop("hist0") if "hist0" in blk else hist_load(0)}
            fin_i = 0
            pending = []
            for idx, (h, g) in enumerate(insts):
                if idx + 1 < len(insts):
                    slots[idx + 1] = hist_load(idx + 1)
                sl = slots[idx]
                nh = g["nh"]
                qc0, nq, qp0, nqr = g["qc0"], g["nq"], g["qp0"], g["nqr"]
                kts = []
                if g["causal"]:
                    for i in range(nh):
                        kts.append(dict(kind="hist", K=lambda m, i=i: KH[sl][m * 64:(m + 1) * 64, i * 128:(i + 1) * 128],
                                        V=VH[sl][:, i, :], kp0=0, nk=128, c0=qc0, c1=qc0 + nq, rb=[bKH[sl]], rvb=[bVH[sl]],
                                        qts=list(range(g["nqt"]))))
                else:
                    for i0 in range(0, nh, 4):
                        kts.append(dict(kind="chunk", subs=list(range(i0, min(i0 + 4, nh))), rb=[bKH[sl]], rvb=[bVH[sl]]))
                if g["causal"]:
                    for j in range(nt):
                        kts.append(dict(kind="diag", j=j, K=lambda m, j=j: KT[m * 64:(m + 1) * 64, h, j * 128:(j + 1) * 128],
                                        V=VA[:, h, j, :], kp0=0, nk=128, c0=j * 128, c1=N, rb=[bKT], rvb=[bVA[j]],
                                        qts=list(range(j, nt))))
                else:
                    k0, nk = g["cur"]
                    kts.append(dict(kind="cur", K=lambda m, k0=k0, nk=nk: KT[m * 64:(m + 1) * 64, h, k0:k0 + nk],
                                    V=VA[k0:k0 + nk, h, 0, :], kp0=k0, nk=nk, c0=qc0, c1=qc0 + nq, rb=[bKT], rvb=[bVA[0]],
                                    qts=[0]))
                started = set()

                def qk(kt, sbk):
                    STB = PS[:, sbk * 512:sbk * 512 + 1024].rearrange("p (m c) -> p m c", m=2)
                    if kt["kind"] == "chunk":
                        for jj, i in enumerate(kt["subs"]):
                            for m in range(2):
                                first = (jj == 0)
                                op("pe", "matmul", out=STB[0:128, m, jj * 128 + qc0:jj * 128 + qc0 + nq],
                                   lhsT=KH[sl][m * 64:(m + 1) * 64, i * 128:(i + 1) * 128], rhs=QT[m * 64:(m + 1) * 64, h, qc0:qc0 + nq],
                                   start=True, stop=True, reads=kt["rb"] + [bQT[h]],
                                   writes=[bBK[sbk + m]] if first else [], adds=[] if first else [bBK[sbk + m]])
                        return STB
                    for m in range(2):
                        op("pe", "matmul", out=STB[kt["kp0"]:kt["kp0"] + kt["nk"], m, kt["c0"]:kt["c1"]], lhsT=kt["K"](m), rhs=QT[m * 64:(m + 1) * 64, h, kt["c0"]:kt["c1"]], start=True, stop=True,
                           reads=kt["rb"] + [bQT[h]], writes=[bBK[sbk + m]])
                    return STB

                def ex(kt, sbk, STB):
                    if kt["kind"] == "chunk":
                        pi = st_cnt["pt"] % NPT
                        st_cnt["pt"] += 1
                        Pt, pb = PT[pi], bPT[pi]
                        ns = len(kt["subs"])
                        op("act", "activation", out=Pt[:, :, :].rearrange("p m (j c) -> p m j c", j=4)[:, :, 0:ns, qc0:qc0 + nq],
                           in_=STB[:, :, :].rearrange("p m (j c) -> p m j c", j=4)[:, :, 0:ns, qc0:qc0 + nq], func=AF.Exp, scale=0.125,
                           reads=[bBK[sbk], bBK[sbk + 1]], writes=[pb] + bPT2[pi])
                        return Pt, pb
                    r0, r1 = kt["kp0"], kt["kp0"] + kt["nk"]
                    wl_extra = []
                    if kt["kind"] == "diag":
                        j = kt["j"]
                        Pt, pb = PD[j], bPD[j]
                        pieces = []
                        if (j + 1) * 128 < N:
                            pieces.append((0, 128, (j + 1) * 128, N))
                        pieces.append((0, 64, j * 128, (j + 1) * 128))
                        pieces.append((64, 128, j * 128 + 64, (j + 1) * 128))
                    else:
                        pi = st_cnt["pt"] % NPT
                        st_cnt["pt"] += 1
                        Pt, pb = PT[pi], bPT[pi]
                        pieces = [(r0, r1, kt["c0"], kt["c1"])]
                        wl_extra = bPT2[pi]
                        if kt["kind"] == "hist" and kt["c1"] - kt["c0"] == 512:
                            for hf in range(2):
                                op("act", "activation", out=Pt[r0:r1, :, hf * 256:(hf + 1) * 256], in_=STB[r0:r1, :, hf * 256:(hf + 1) * 256],
                                   func=AF.Exp, scale=0.125, reads=[bBK[sbk], bBK[sbk + 1]], writes=[bPT2[pi][hf], pb] if hf == 0 else [bPT2[pi][hf]],
                                   adds=[] if hf == 0 else [pb])
                            return Pt, (pb, bPT2[pi])
                    for pi_, (a0, a1, c0, c1) in enumerate(pieces):
                        op("act", "activation", out=Pt[a0:a1, :, c0:c1], in_=STB[a0:a1, :, c0:c1], func=AF.Exp, scale=0.125,
                           reads=[bBK[sbk], bBK[sbk + 1]], writes=([pb] + wl_extra) if pi_ == 0 else [], adds=[] if pi_ == 0 else [pb])
                    return Pt, pb

                def pv(kt, Pt, pb, last):
                    if kt["kind"] == "chunk":
                        for jj, i in enumerate(kt["subs"]):
                            for m in range(2):
                                bankid = 4 + m * 2
                                key = (bankid, qp0)
                                stt = key not in started
                                started.add(key)
                                op("pe", "matmul", out=OACC[0:128, m, g["oslot"], 0:129], lhsT=Pt[0:128, m, jj * 128:(jj + 1) * 128],
                                   rhs=VH[sl][:, i, :], start=stt, stop=False, skip_group_check=True,
                                   reads=[pb] + kt["rvb"], writes=[bBK[bankid]] if stt else [], inc=True)
                                if not stt:
                                    bBK[bankid].w = [("E_pe", S_.cnt["E_pe"])]
                        return
                    if isinstance(pb, tuple):
                        pb_all, pb_h = pb
                    else:
                        pb_all, pb_h = pb, None
                    for qt in kt["qts"]:
                        pb = pb_h[qt // 2] if pb_h is not None else pb_all
                        for m in range(2):
                            bankid = 4 + m * 2 + qt // 2
                            key = (bankid, qp0)
                            stt = key not in started
                            started.add(key)
                            if g["causal"]:
                                o_ap = OACC[0:128, m, qt, 0:129]
                                l_ap = Pt[kt["kp0"]:kt["kp0"] + kt["nk"], m, qt * 128:(qt + 1) * 128]
                            else:
                                o_ap = OACC[0:128, m, g["oslot"], 0:129]
                                l_ap = Pt[kt["kp0"]:kt["kp0"] + kt["nk"], m, 0:128]
                            op("pe", "matmul", out=o_ap, lhsT=l_ap, rhs=kt["V"], start=stt, stop=last, skip_group_check=True,
                               reads=[pb] + kt["rvb"], writes=[bBK[bankid]] if stt else [], inc=True)
                            if not stt:
                                bBK[bankid].w = [("E_pe", S_.cnt["E_pe"])]

                sbks = [0, 2]
                cur = qk(kts[0], sbks[0])
                for i, kt in enumerate(kts):
                    nxt = None
                    if i + 1 < len(kts):
                        nxt = qk(kts[i + 1], sbks[(i + 1) % 2])
                    Pt, pb = ex(kt, sbks[i % 2], cur)
                    pv(kt, Pt, pb, i == len(kts) - 1)
                    cur = nxt
                    if pending and (i == 2 or i == len(kts) - 1):
                        pending.pop(0)()
                if g is blk["groups"][-1]:
                    fi = fin_i % 2
                    fin_i += 1
                    obk = [bBK[4], bBK[5], bBK[6], bBK[7]]
                    if var == 0:
                        op("dve", "tensor_copy", out=OS[fi][:, :, 0:nt, :], in_=OACC[:, :, 0:nt, 0:129], reads=obk, writes=[bOS[fi]])
                    else:
                        op("dve", "tensor_copy", out=OS[fi][0:64, :, 0, :], in_=OACC[0:64, :, 0, 0:129], reads=obk, writes=[bOS[fi]])
                        op("dve", "tensor_copy", out=OS[fi][64:128, :, 0, :], in_=OACC[64:128, :, 1, 0:129], reads=obk, adds=[bOS[fi]])

                    def fin_rest(h=h, fi=fi):
                        op("dve", "reciprocal", out=RL[:, :, 0:nt], in_=OS[fi][:, :, 0:nt, 128], reads=[bOS[fi]], writes=[bRL])
                        op("dve", "tensor_tensor", out=OS[fi][:, :, 0:nt, 0:128], in0=OS[fi][:, :, 0:nt, 0:128],
                           in1=RL[:, :, 0:nt].unsqueeze(3).to_broadcast([128, 2, nt, 128]), op=ALU.mult, reads=[bRL], writes=[bOS[fi]])
                        op("dve", "scalar_tensor_tensor", out=OA1[:, 0:nt, :], in0=OS[fi][:, 1, 0:nt, 0:128], scalar=NLAM[:, L:L + 1],
                           in1=OS[fi][:, 0, 0:nt, 0:128], op0=ALU.mult, op1=ALU.add, reads=[bOS[fi], bC], writes=[bOA])
                        for qt in range(nt):
                            op("act", "activation", out=OS[fi][:, 1, qt, 0:128], in_=OA1[:, qt, :], func=AF.Square, accum_out=SSO[:, h, qt:qt + 1],
                               reads=[bOA], writes=[bSSOh[h], bOS[fi]] if qt == 0 else [],
                               adds=[] if qt == 0 else [bSSOh[h], bOS[fi]])
                        rsqrt(SSO[:, h, 0:nt], RSO[:, h, 0:nt], None, 1.0 / 128, EPS, [bSSOh[h]], [bSSOh[h]])
                        for qt in range(nt):
                            first = (h == 0 and qt == 0)
                            op("dve", "scalar_tensor_tensor", out=YB[:, qt, h * 128:(h + 1) * 128], in0=OA1[:, qt, :], scalar=RSO[:, h, qt:qt + 1],
                               in1=GZ[:, qt, h * 128:(h + 1) * 128], op0=ALU.mult, op1=ALU.mult,
                               reads=[bOA, bSSOh[h], bGZ[qt]], writes=(bYB[0:nt] + bHT) if first else [], adds=[] if first else [bYB[qt]])
                    pending.append(fin_rest)
            while pending:
                pending.pop(0)()


        def fm_unit_safe(slot, j, rhs_of_kc, rbufs, N):
            b = bank()
            for kc in range(8):
                rb = [bW[slot]] + rbufs(kc)
                if kc == 7:
                    for k2 in range(7):
                        rb = rb + rbufs(k2)
                op("pe", "matmul", out=bk(b, N), lhsT=W[slot][:, kc, j * 128:(j + 1) * 128], rhs=rhs_of_kc(kc), start=(kc == 0), stop=(kc == 7),
                   reads=rb, writes=[bBK[b]], inc=(kc == 7 or kc == 0))
            return b

        def tm_pair(slots, t, lhs_of_kc, lbufs):
            b = pair()
            for u2 in range(2):
                for kc in range(8):
                    rb = [bW[slots[u2]]] + lbufs(kc, t)
                    if kc == 7:
                        for k2 in range(7):
                            rb = rb + lbufs(k2, t)
                    op("pe", "matmul", out=bk(b + u2), lhsT=lhs_of_kc(kc, t), rhs=W[slots[u2]][:, kc, :], start=(kc == 0), stop=(kc == 7),
                       reads=rb, writes=[bBK[b + u2]], inc=(kc == 7 or kc == 0))
            return b

        def do_block(blk, bi, nblocks_total):
            nt = blk["nt"]
            N = nt * 128
            var = blk["var"]
            first = (bi == 0)
            nmax = nblocks_total * 48 - 1
            base = bi * 48
            for t in range(nt):
                op("sp", "dma_start", out=X[:, t, :], in_=blk["x"][:, t, :], writes=[bX[t]], dma="x%d" % t)
            for L in range(2):
                ub = base + L * 24
                op("sp", "dma_start", out=GVB[:], in_=v_norm_g[L].partition_broadcast(128), writes=[bGV], dma="gv")
                for t in range(nt):
                    i = st_cnt["i2"] % 2
                    st_cnt["i2"] += 1
                    op("act", "activation", out=SB16[i][:], in_=X[:, t, :], func=AF.Square, accum_out=SSQ[:, t:t + 1],
                       reads=[bX[t]], writes=[bSB16[i]] + ([bSSQ] if t == 0 else []), adds=[] if t == 0 else [bSSQ])
                rsqrt(SSQ[:, 0:nt], RSQ[:, 0:nt], None, 1.0 / D, EPS, [bSSQ], [bSSQ])
                run_pre(blk["pre"] if L == 0 else blk["pre_b"])
                tb = [bank() for _ in range(4)]
                for t in range(nt):
                    i = st_cnt["i2"] % 2
                    st_cnt["i2"] += 1
                    op("dve", "tensor_scalar", out=SB16[i][:], in0=X[:, t, :], scalar1=RSQ[:, t:t + 1], scalar2=None, op0=ALU.mult, reads=[bX[t], bSSQ], writes=[bSB16[i]])
                    for kc in range(8):
                        b = tb[kc // 2]
                        c0 = b * 1024 + (kc % 2) * 512 + t * 128
                        op("pe", "transpose", out=PSB[:, c0:c0 + 128], in_=SB16[i][:, kc * 128:(kc + 1) * 128], identity=IDB[:],
                           reads=[bSB16[i], bC], writes=[bBK[b]], inc=True)
                for kc in range(8):
                    b = tb[kc // 2]
                    c0 = b * 1024 + (kc % 2) * 512
                    if kc % 2 == 0:
                        op("act", "activation", out=HT[:, kc, 0:N], in_=PSB[:, c0:c0 + N], func=AF.Copy, scale=NG[:, L, kc:kc + 1],
                           reads=[bBK[b], bC], writes=[bHT[kc]] + bYB)
                    else:
                        op("dve", "tensor_scalar", out=HT[:, kc, 0:N], in0=PSB[:, c0:c0 + N], scalar1=NG[:, L, kc:kc + 1], scalar2=None, op0=ALU.mult,
                           reads=[bBK[b], bC], writes=[bHT[kc]] + bYB)

                ckpt("ht%d" % L)

                def rhsHT(kc):
                    return HT[:, kc, 0:N]

                def rbHT(kc):
                    return [bHT[kc]]

                def lhsHT(kc, t):
                    return HT[:, kc, t * 128:(t + 1) * 128]

                def lbHT(kc, t):
                    return [bHT[kc]]

                def gelu_chain(src, n, i, rbk):
                    op("act", "activation", out=TA[i][:, 0:n], in_=src, func=AF.Square, reads=rbk, writes=[bTA[i]])
                    op("dve", "scalar_tensor_tensor", out=TA[i][:, 0:n], in0=TA[i][:, 0:n], scalar=1.0 / 0.044715, in1=src,
                       op0=ALU.add, op1=ALU.mult, reads=rbk, writes=[bTA[i]])
                    op("act", "activation", out=TBF[i][:, 0:n], in_=TA[i][:, 0:n], func=AF.Tanh, scale=KGELU * 0.044715,
                       reads=[bTA[i]], writes=[bTB[i]])

                ckpt("z%d" % L)
                s0 = wneed(ub + 0, nmax, first)
                s1 = wneed(ub + 1, nmax, first, live=ub + 0)
                for tp0 in range(0, nt, 2):
                    tl = list(range(tp0, min(tp0 + 2, nt)))
                    info = {}
                    for t in tl:
                        b = tm_pair((s0, s1), t, lhsHT, lbHT)
                        info[t] = (b, t % 2, PS[:, b * 512:b * 512 + 1024], [bBK[b], bBK[b + 1]])
                    for t in tl:
                        b, i, src, rbk = info[t]
                        op("act", "activation", out=TA[i][:], in_=src, func=AF.Square, reads=rbk, writes=[bTA[i]])
                    for t in tl:
                        b, i, src, rbk = info[t]
                        op("dve", "scalar_tensor_tensor", out=TA[i][:], in0=TA[i][:], scalar=1.0 / 0.044715, in1=src,
                           op0=ALU.add, op1=ALU.mult, reads=rbk, writes=[bTA[i]])
                    for t in tl:
                        b, i, src, rbk = info[t]
                        op("act", "activation", out=TBF[i][:], in_=TA[i][:], func=AF.Tanh, scale=KGELU * 0.044715,
                           reads=[bTA[i]], writes=[bTB[i]])
                    for t in tl:
                        b, i, src, rbk = info[t]
                        op("dve", "scalar_tensor_tensor", out=TA[i][:], in0=TBF[i][:], scalar=1.0, in1=src, op0=ALU.add, op1=ALU.mult,
                           reads=[bTB[i]] + rbk, writes=[bTA[i]])
                    for k2, t in enumerate(tl):
                        b, i, src, rbk = info[t]
                        op("act", "activation", out=TBF[i][:], in_=TA[i][:], func=AF.Square, accum_out=SSV[:, t:t + 1],
                           reads=[bTA[i]], writes=[bTB[i]] + ([bSSV] if k2 == 0 else []), adds=[] if k2 == 0 else [bSSV])
                    rsqrt(SSV[:, tl[0]:tl[-1] + 1], RSV[:, tl[0]:tl[-1] + 1], None, 1.0 / D, 4.0 * EPS, [bSSV], [bSSV])
                    for t in tl:
                        b, i, src, rbk = info[t]
                        if var == 0:
                            op("dve", "scalar_tensor_tensor", out=VN[:, t, :], in0=TA[i][:], scalar=RSV[:, t:t + 1], in1=GV[L][:], op0=ALU.mult, op1=ALU.mult,
                               reads=[bTA[i], bSSV, bC, bGV], writes=[bVN[t], bKT] + bMBT)
                        else:
                            k_ = st_cnt["ev"] % 2
                            st_cnt["ev"] += 1
                            op("dve", "scalar_tensor_tensor", out=STG[k_][:], in0=TA[i][:], scalar=RSV[:, t:t + 1], in1=GV[L][:], op0=ALU.mult, op1=ALU.mult,
                               reads=[bTA[i], bSSV, bC, bGV], writes=[bSTG[k_]])
                            op("sp", "dma_start", out=nms[L], in_=STG[k_][:], reads=[bSTG[k_]], dma="stg%d" % k_)
                            op("dve", "tensor_copy", out=VN[:, t, :], in_=STG[k_][:],
                               reads=[bSTG[k_]], writes=[bVN[t], bKT] + bMBT)
                for u in range(2):
                    slot = wneed(ub + 2 + u, nmax, first)
                    for j in range(4):
                        cg = u * 4 + j
                        b = fm_unit_safe(slot, j, rhsHT, rbHT, N)
                        i = st_cnt["i2"] % 2
                        st_cnt["i2"] += 1
                        gelu_chain(bk(b, N), N, i, [bBK[b]])
                        op("dve", "scalar_tensor_tensor", out=UT[:, cg, 0:N], in0=TBF[i][:, 0:N], scalar=1.0, in1=bk(b, N), op0=ALU.add, op1=ALU.mult,
                           reads=[bTB[i], bBK[b]], writes=[bUT[cg]])
                ckpt("u%d" % L)
                for u in range(2):
                    slot = wneed(ub + 4 + u, nmax, first)
                    for j in range(4):
                        cg = u * 4 + j
                        b = fm_unit_safe(slot, j, rhsHT, rbHT, N)
                        i = st_cnt["i2"] % 2
                        st_cnt["i2"] += 1
                        op("act", "activation", out=TBF[i][:, 0:N], in_=bk(b, N), func=AF.Tanh, scale=0.5,
                           reads=[bBK[b]], writes=[bTB[i]])
                        op("dve", "scalar_tensor_tensor", out=TA[i][:, 0:N], in0=TBF[i][:, 0:N], scalar=1.0, in1=bk(b, N), op0=ALU.add, op1=ALU.mult,
                           reads=[bTB[i], bBK[b]], writes=[bTA[i]])
                        op("pool", "tensor_tensor", out=UT[:, cg, 0:N], in0=UT[:, cg, 0:N], in1=TA[i][:, 0:N], op=ALU.mult, reads=[bTA[i]], writes=[bUT[cg]])
                ckpt("sp%d" % L)
                for u in range(2):
                    slot = wneed(ub + 6 + u, nmax, first)
                    for j in range(4):
                        cg = u * 4 + j
                        b = fm_unit_safe(slot, j, rhsHT, rbHT, N)
                        op("act", "activation", out=SG[:, cg, 0:N], in_=bk(b, N), func=AF.Tanh, scale=0.5,
                           reads=[bBK[b]], writes=[bSG[cg]])
                ckpt("v%d" % L)
                for cg in range(8):
                    g = cg // 2
                    b = bank()
                    for t in range(nt):
                        op("pe", "matmul", out=bk(b, 128, t * 128), lhsT=ONES[:, :], rhs=BR[:, var, L, g, :], start=(t == 0), stop=False, skip_group_check=True,
                           reads=[bC], writes=[bBK[b]], inc=(t == 0))
                    for t in range(nt):
                        op("pe", "matmul", out=bk(b, 128, t * 128), lhsT=VN[:, t, cg * 128:(cg + 1) * 128], rhs=WST[:, var, L, g, :], start=False, stop=(t == nt - 1), skip_group_check=True,
                           reads=[bVN[t], bC], writes=[bBK[b]] if t == nt - 1 else [], inc=(t == nt - 1))
                    op("dve", "scalar_tensor_tensor", out=UT[:, cg, 0:N], in0=bk(b, N), scalar=0.25, in1=UT[:, cg, 0:N], op0=ALU.mult, op1=ALU.mult,
                       reads=[bBK[b]], writes=[bUT[cg]])
                for u in range(2):
                    slot = wneed(ub + 8 + u, nmax, first)
                    for j in range(4):
                        cg = u * 4 + j
                        b = fm_unit_safe(slot, j, lambda kc: UT[:, kc, 0:N], lambda kc: [bUT[kc]], N)
                        op("dve", "scalar_tensor_tensor", out=MT[:, cg, 0:N], in0=SG[:, cg, 0:N], scalar=1.0, in1=bk(b, N), op0=ALU.add, op1=ALU.mult,
                           reads=[bSG[cg], bBK[b]], writes=[bMT[cg]])
                ckpt("pa%d" % L)
                blk["hist0"] = hist_load_inst(L, 0, blk["groups"][0])
                ckpt("q%d" % L)
                s0 = wneed(ub + 10, nmax, first)
                s1 = wneed(ub + 11, nmax, first, live=ub + 10)
                kpend = []
                for t in range(nt):
                    b = tm_pair((s0, s1), t, lhsHT, lbHT)
                    src = PS[:, b * 512:b * 512 + 1024]
                    k_ = st_cnt["ev"] % 2
                    st_cnt["ev"] += 1
                    i = st_cnt["i2"] % 2
                    st_cnt["i2"] += 1
                    ckpt("ka%d" % L)
                    op("act", "activation", out=STG[k_][:], in_=src, func=AF.Copy,
                       reads=[bBK[b], bBK[b + 1]], writes=[bSTG[k_]])
                    ckpt("kb%d" % L)
                    op("dve", "tensor_copy", out=SB16[i][:], in_=STG[k_][:],
                       reads=[bSTG[k_]], writes=[bSB16[i]])
                    ckpt("kc%d" % L)
                    op("sp", "dma_start", out=blk["nk"][L][t * 128:(t + 1) * 128, :], in_=STG[k_][:],
                       reads=[bSTG[k_]], dma="stg%d" % k_)
                    def ktr(t=t, i=i):
                        b2 = bank()
                        for hh in range(H):
                            op("pe", "transpose", out=PSB[:, b2 * 1024 + hh * 128:b2 * 1024 + (hh + 1) * 128], in_=SB16[i][:, hh * 128:(hh + 1) * 128], identity=IDB[:],
                               reads=[bSB16[i], bC], writes=[bBK[b2]], inc=True)
                        src_ps = PSB[:, b2 * 1024:(b2 + 1) * 1024].rearrange("p (h c) -> p h c", h=H)
                        op("dve", "tensor_copy", out=KT[:, :, t * 128:(t + 1) * 128], in_=src_ps,
                           reads=[bBK[b2]], writes=[bKT] + bVN + bMBT)
                    if kpend:
                        kpend.pop(0)()
                    kpend.append(ktr)
                while kpend:
                    kpend.pop(0)()
                ckpt("ky%d" % L)
                if blk["kv_store"]:
                    bKTS[(L, bi)] = Buf()
                    op("sp", "dma_start", out=KTS[L, :, :, bi * 512:(bi + 1) * 512].rearrange("h p c -> p h c"), in_=KT[:, :, :],
                       reads=[bKT], writes=[bKTS[(L, bi)]], dma="kts%d" % (bi % 2))
                ckpt("k%d" % L)
                s0 = wneed(ub + 12, nmax, first)
                s1 = wneed(ub + 13, nmax, first, live=ub + 12)
                for t in range(nt):
                    b = tm_pair((s0, s1), t, lhsHT, lbHT)
                    src = PS[:, b * 512:b * 512 + 1024]
                    k_ = st_cnt["ev"] % 2
                    st_cnt["ev"] += 1
                    op("act", "activation", out=STG[k_][:], in_=src, func=AF.Copy,
                       reads=[bBK[b], bBK[b + 1]], writes=[bSTG[k_]])
                    op("dve", "tensor_copy", out=VA[:, :, t, 0:128], in_=STG[k_][:].rearrange("p (h c) -> p h c", h=H),
                       reads=[bSTG[k_]], writes=[bVA[t]])
                    op("sp", "dma_start", out=blk["nv"][L][t * 128:(t + 1) * 128, :], in_=STG[k_][:],
                       reads=[bSTG[k_]], dma="stg%d" % k_)
                if blk["kv_store"]:
                    bVSS[(L, bi)] = Buf()
                    op("sp", "dma_start", out=VSS[L, :, :, bi * 4:(bi + 1) * 4, :].rearrange("h p t c -> p h t c"), in_=VA[:],
                       reads=bVA, writes=[bVSS[(L, bi)]], dma="vss%d" % (bi % 2))
                for u in range(2):
                    slot = wneed(ub + 14 + u, nmax, first)
                    for j in range(4):
                        hh = u * 4 + j
                        b = fm_unit_safe(slot, j, rhsHT, rbHT, N)
                        if hh % 2 == 0:
                            op("act", "activation", out=QT[:, hh, 0:N], in_=bk(b, N), func=AF.Copy,
                               reads=[bBK[b]], writes=[bQT[hh]] + bUT)
                        else:
                            op("dve", "tensor_copy", out=QT[:, hh, 0:N], in_=bk(b, N),
                               reads=[bBK[b]], writes=[bQT[hh]] + bUT)
                s0 = wneed(ub + 16, nmax, first)
                s1 = wneed(ub + 17, nmax, first, live=ub + 16)
                for t in range(nt):
                    b = tm_pair((s0, s1), t, lhsHT, lbHT)
                    src = PS[:, b * 512:b * 512 + 1024]
                    i = st_cnt["i2"] % 2
                    st_cnt["i2"] += 1
                    op("act", "activation", out=TBF[i][:], in_=src, func=AF.Tanh, scale=0.5,
                       reads=[bBK[b], bBK[b + 1]], writes=[bTB[i]])
                    op("dve", "scalar_tensor_tensor", out=TA[i][:], in0=TBF[i][:], scalar=1.0, in1=src, op0=ALU.add, op1=ALU.mult,
                       reads=[bTB[i], bBK[b], bBK[b + 1]], writes=[bTA[i]])
                    op("pool", "tensor_tensor", out=GZ[:, t, :].rearrange("p (h c) -> p h c", h=H), in0=TA[i][:].rearrange("p (h c) -> p h c", h=H), in1=AG[L][:].unsqueeze(1).to_broadcast([128, H, 128]), op=ALU.mult,
                       reads=[bTA[i], bC], writes=[bGZ[t]])
                for u in range(2):
                    slot = wneed(ub + 18 + u, nmax, first)
                    for j in range(4):
                        cg = u * 4 + j
                        b = fm_unit_safe(slot, j, rhsHT, rbHT, N)
                        op("act", "activation", out=SG[:, cg, 0:N], in_=bk(b, N), func=AF.Tanh, scale=0.5,
                           reads=[bBK[b]], writes=[bSG[cg]])
                ckpt("gb%d" % L)
                attention(L, blk)
                ckpt("at%d" % L)
                tb = [bank() for _ in range(4)]
                for t in range(nt):
                    for hh in range(H):
                        b = tb[hh // 2]
                        c0 = b * 1024 + (hh % 2) * 512 + t * 128
                        op("pe", "transpose", out=PSB[:, c0:c0 + 128], in_=YB[:, t, hh * 128:(hh + 1) * 128], identity=IDB[:],
                           reads=[bYB[t], bC], writes=[bBK[b]], inc=True)
                for hh in range(H):
                    b = tb[hh // 2]
                    c0 = b * 1024 + (hh % 2) * 512
                    if hh % 2 == 0:
                        op("act", "activation", out=YBT[:, hh, 0:N], in_=PSB[:, c0:c0 + N], func=AF.Copy,
                           reads=[bBK[b]], writes=[bUT[hh]] + bQT)
                    else:
                        op("dve", "tensor_copy", out=YBT[:, hh, 0:N], in_=PSB[:, c0:c0 + N],
                           reads=[bBK[b]], writes=[bUT[hh]] + bQT)
                for u in range(2):
                    slot = wneed(ub + 20 + u, nmax, first)
                    for j in range(4):
                        cg = u * 4 + j
                        b = fm_unit_safe(slot, j, lambda kc: YBT[:, kc, 0:N], lambda kc: [bUT[kc]], N)
                        i = st_cnt["i2"] % 2
                        st_cnt["i2"] += 1
                        op("dve", "scalar_tensor_tensor", out=TA[i][:, 0:N], in0=SG[:, cg, 0:N], scalar=1.0, in1=bk(b, N), op0=ALU.add, op1=ALU.mult,
                           reads=[bSG[cg], bBK[b]], writes=[bTA[i]])
                        op("pool", "tensor_tensor", out=MBT[:, cg, 0:N], in0=TA[i][:, 0:N], in1=MT[:, cg, 0:N], op=ALU.add,
                           reads=[bTA[i], bMT[cg]], writes=[bMBT[cg], bKT] + bVN)
                ckpt("pb%d" % L)
                s0 = wneed(ub + 22, nmax, first)
                s1 = wneed(ub + 23, nmax, first, live=ub + 22)
                for t in range(nt):
                    b = tm_pair((s0, s1), t, lambda kc, t: MBT[:, kc, t * 128:(t + 1) * 128], lambda kc, t: [bMBT[kc]])
                    src = PS[:, b * 512:b * 512 + 1024]
                    for u2 in range(2):
                        op("dve", "scalar_tensor_tensor", out=X[:, t, u2 * 512:(u2 + 1) * 512], in0=bk(b + u2), scalar=0.5,
                           in1=X[:, t, u2 * 512:(u2 + 1) * 512], op0=ALU.mult, op1=ALU.add,
                           reads=[bBK[b + u2]], writes=[bX[t]] if u2 == 0 else [], adds=[] if u2 == 0 else [bX[t]])
            op("sp", "dma_start", out=GVB[:], in_=final_g.partition_broadcast(128), writes=[bGV], dma="gv")
            for t in range(nt):
                i = st_cnt["i2"] % 2
                st_cnt["i2"] += 1
                op("act", "activation", out=SB16[i][:], in_=X[:, t, :], func=AF.Square, accum_out=SSQ[:, t:t + 1],
                   reads=[bX[t]], writes=[bSB16[i]] + ([bSSQ] if t == 0 else []), adds=[] if t == 0 else [bSSQ])
            rsqrt(SSQ[:, 0:nt], RSQ[:, 0:nt], None, 1.0 / D, EPS, [bSSQ], [bSSQ])
            for t in range(nt):
                k_ = st_cnt["ev"] % 2
                st_cnt["ev"] += 1
                op("dve", "scalar_tensor_tensor", out=STG[k_][:], in0=X[:, t, :], scalar=RSQ[:, t:t + 1], in1=FG[:], op0=ALU.mult, op1=ALU.mult,
                   reads=[bX[t], bSSQ, bC, bGV], writes=[bSTG[k_]])
                op("sp", "dma_start", out=blk["y"][t * 128:(t + 1) * 128, :], in_=STG[k_][:],
                   reads=[bSTG[k_]], dma="stg%d" % k_)

        def prompt_hist(bi):
            def f(L, h):
                nh = bi * 4
                kb = [bKTS[(L, b_)] for b_ in range(bi)]
                vb = [bVSS[(L, b_)] for b_ in range(bi)]
                return KTS[L, h, :, 0:nh * 128], kb, VSS[L, h, :, 0:nh, :], vb
            return f

        def sample_hist(s):
            def f(L, h):
                kb = [bKTC[k_] for k_ in bKTC if k_[0] == L and k_[1] == s]
                vb = [bVC[(L, s, h)]]
                return KTC[L, s, h, :, :], kb, VC[L, s, h, :, :, :], vb
            return f

        blocks = []
        for bi in range(NB):
            blocks.append(dict(nt=4, var=0, x=xp[bi * 512:(bi + 1) * 512, :].rearrange("(t p) d -> p t d", p=128),
                               y=yp[bi * 512:(bi + 1) * 512, :],
                               nk=[nkp[L, bi * 512:(bi + 1) * 512, :] for L in range(2)],
                               nv=[nvp[L, bi * 512:(bi + 1) * 512, :] for L in range(2)],
                               kv_store=(bi < NB - 1),
                               groups=[dict(qc0=0, nq=512, qp0=0, nqr=128, nqt=4, nh=bi * 4, hist=prompt_hist(bi), causal=True)]))
        blocks.append(dict(nt=1, var=1, x=xs.rearrange("(t p) d -> p t d", p=128), y=ys,
                           nk=[nks[L] for L in range(2)], nv=[nvs[L] for L in range(2)], kv_store=False,
                           groups=[dict(qc0=s * 64, nq=64, qp0=s * 64, nqr=64, nqt=1, nh=NTP, hist=sample_hist(s), causal=False, oslot=s,
                                        cur=(s * 64, 64)) for s in range(2)]))
        nbt = len(blocks)
        v_items = [it for it in pre_items if it[0] == "v"]
        k_items = [it for it in pre_items if it[0] == "k"]
        pv_per = (len(v_items) + NB - 1) // NB
        pk_per = (len(k_items) + NB - 1) // NB
        try:
            for bi, blk in enumerate(blocks):
                kk = k_items[bi * pk_per:(bi + 1) * pk_per] if bi < NB else []
                blk["pre"] = (v_items[bi * pv_per:(bi + 1) * pv_per] if bi < NB else []) + kk[0:(len(kk) + 1) // 2]
                blk["pre_b"] = kk[(len(kk) + 1) // 2:]
                do_block(blk, bi, nbt)
        except _Stop:
            pass
        S_.emit()
    return nc


_CACHE = {}


def _consts():
    ident = np.eye(128, dtype=np.float32)
    t = np.arange(128)
    tril0 = (t[None, :] <= t[:, None]).astype(np.float32)
    same = (t[None, :] // 64) == (t[:, None] // 64)
    tril1 = (tril0 * same).astype(np.float32)
    return ident, np.stack([tril0, tril1])


def kernel(x_prompt, x_sample, cache_k, cache_v, norm_g, w_in, w_s, b_s, v_norm_g, lam_q1, lam_k1, lam_q2, lam_k2,
           attn_norm_g, w_pa, w_pb, w_out, final_norm_g):
    f = lambda a: np.ascontiguousarray(np.asarray(a, dtype=np.float32))
    x_prompt, x_sample, cache_k, cache_v = f(x_prompt), f(x_sample), f(cache_k), f(cache_v)
    B, S, _ = x_prompt.shape
    DB, DS, _ = x_sample.shape
    P = cache_k.shape[2]
    ncores = B
    assert DB == 2 * ncores and DS == 64
    key = (S, P)
    if key not in _CACHE:
        _CACHE[key] = build(S, P)
    nc = _CACHE[key]
    ident, tril = _consts()
    shared = {"norm_g": f(norm_g), "w_in": f(w_in), "w_pa": f(w_pa), "w_pb": f(w_pb), "w_out": f(w_out), "w_s": f(w_s),
              "b_s": f(b_s), "v_norm_g": f(v_norm_g), "lam_q1": f(lam_q1), "lam_k1": f(lam_k1), "lam_q2": f(lam_q2),
              "lam_k2": f(lam_k2), "attn_norm_g": f(attn_norm_g), "final_norm_g": f(final_norm_g), "ident": ident, "tril": tril}
    in_maps = []
    for c in range(ncores):
        m = dict(shared)
        m["xp"] = x_prompt[c]
        m["xs"] = x_sample[2 * c:2 * c + 2].reshape(128, D)
        m["ck"] = np.ascontiguousarray(cache_k[:, 2 * c:2 * c + 2].reshape(2, 2, P, D))
        m["cv"] = np.ascontiguousarray(cache_v[:, 2 * c:2 * c + 2].reshape(2, 2, P, D))
        in_maps.append(m)
    res = run_bass_kernel_spmd(nc, in_maps, core_ids=list(range(ncores)))
    r = res.results
    y_prompt = np.stack([r[c]["yp"] for c in range(ncores)], 0)
    y_sample = np.concatenate([r[c]["ys"].reshape(2, 64, D) for c in range(ncores)], 0)
    nkp = np.stack([r[c]["nkp"] for c in range(ncores)], 1).reshape(2, B, S, H, 128)
    nvp = np.stack([r[c]["nvp"] for c in range(ncores)], 1).reshape(2, B, S, H, 128)
    nks = np.concatenate([r[c]["nks"].reshape(2, 2, 64, H, 128) for c in range(ncores)], 1)
    nvs = np.concatenate([r[c]["nvs"].reshape(2, 2, 64, H, 128) for c in range(ncores)], 1)
    nms = np.concatenate([r[c]["nms"].reshape(2, 2, 64, D) for c in range(ncores)], 1)
    return (y_prompt.astype(np.float32), y_sample.astype(np.float32), nkp.astype(np.float32), nvp.astype(np.float32),
            nks.astype(np.float32), nvs.astype(np.float32), nms.astype(np.float32))
```
